# Optimizing a Trainium2 kernel written in Bass

```python
import jax
import jax.numpy as jnp
from jax import lax
import numpy as np

D_MODEL = 1024
BATCH = 2
SEQ = 8192
DEPTH = 2

N_META = 16
D_MIX = D_MODEL
CONV_WIDTH = D_MIX // 4
HG_WIDTH = D_MIX // 2
HG_HEAD_DIM = 128
HG_HEADS = HG_WIDTH // HG_HEAD_DIM
POOL_WIDTH = D_MIX - CONV_WIDTH - HG_WIDTH
POOL_WINDOWS = (2, 4, 8, 16)
POOL_GROUPS = len(POOL_WINDOWS)
POOL_GROUP_DIM = POOL_WIDTH // POOL_GROUPS
SHORT_CONV_K = 3
FFN_CONV_K = 3
D_FF = 2816
CHUNK = 64
D_IN = 3 * CONV_WIDTH + 4 * HG_WIDTH + POOL_WIDTH
ALPHA = (2 * DEPTH) ** 0.25
BETA = (8 * DEPTH) ** -0.25
LN_EPS = 1e-5
RMS_EPS = 1e-6
F_FLOOR = 1e-30
SPLIT_SIZES = (CONV_WIDTH,) * 3 + (HG_WIDTH,) * 4 + (POOL_WIDTH,)
SPLIT_IDX = tuple(int(s) for s in np.cumsum(SPLIT_SIZES)[:-1])

kernel_name = "hymba_conv_hgrn2_pool_deepnorm"


def causal_dwconv(x, w, b=None):
    K = w.shape[-1]
    L = x.shape[1]
    xp = jnp.pad(x, ((0, 0), (K - 1, 0), (0, 0)))
    y = xp[:, 0:L, :] * w[:, 0]
    for k in range(1, K):
        y = y + xp[:, k:k + L, :] * w[:, k]
    if b is not None:
        y = y + b
    return y


def layer_norm(x, g, b):
    xf = x.astype(jnp.float32)
    mu = jnp.mean(xf, axis=-1, keepdims=True)
    var = jnp.mean(jnp.square(xf - mu), axis=-1, keepdims=True)
    return ((xf - mu) * lax.rsqrt(var + LN_EPS) * g + b).astype(x.dtype)


def short_conv_mixer(bg, cg, v, w_conv):
    return bg * causal_dwconv(cg * v, w_conv)


def multiscale_pool_mixer(v, w_pool, pool_scale):
    B, L, _ = v.shape
    vf = v.astype(jnp.float32)
    c = jnp.pad(jnp.cumsum(vf, axis=1), ((0, 0), (1, 0), (0, 0)))
    t = jnp.arange(L)
    outs = []
    for gi, win in enumerate(POOL_WINDOWS):
        lo, hi = gi * POOL_GROUP_DIM, (gi + 1) * POOL_GROUP_DIM
        cg = c[..., lo:hi]
        prev = jnp.pad(cg, ((0, 0), (win, 0), (0, 0)))[:, 1:L + 1]
        count = jnp.minimum(t + 1, win).astype(jnp.float32)[:, None]
        outs.append((cg[:, 1:] - prev) / count - vf[..., lo:hi])
    d = jnp.stack(outs, axis=2)
    y = jnp.einsum('blgc,gcd->blgd', d, w_pool).reshape(B, L, POOL_WIDTH)
    return (y * pool_scale).astype(v.dtype)


def hgrn2_mixer(q, fz, i, gz, lb, g_norm):
    B, L, _ = q.shape
    f32 = jnp.float32
    fz = fz.astype(f32)
    lb = lb.astype(f32)
    sig = jax.nn.sigmoid(fz)
    f = lb + (1.0 - lb) * sig
    log_f = jnp.log(jnp.maximum(f, F_FLOOR))
    k = (1.0 - lb) * (1.0 - sig)
    pad = CHUNK - N_META

    def to_chunks(a):
        a = jnp.pad(a.astype(f32), ((0, 0), (pad, 0), (0, 0)))
        n = a.shape[1] // CHUNK
        return a.reshape(B, n, CHUNK, HG_HEADS, HG_HEAD_DIM).transpose(1, 0, 3, 2, 4)

    qc = to_chunks(q.astype(f32) * (HG_HEAD_DIM ** -0.5))
    kc, ic, gc = to_chunks(k), to_chunks(i), to_chunks(log_f)
    mask = jnp.tril(jnp.ones((CHUNK, CHUNK), dtype=bool))[:, :, None]

    def step(S, xs):
        qb, kb, ib, gb = xs
        G = jnp.cumsum(gb, axis=2)
        o_inter = jnp.einsum('bhtd,bhde->bhte', qb * jnp.exp(G), S)
        diff = G[:, :, :, None, :] - G[:, :, None, :, :]
        decay = jnp.where(mask, jnp.exp(jnp.where(mask, diff, 0.0)), 0.0)
        A = jnp.einsum('bhtd,bhsd,bhtsd->bhts', qb, kb, decay)
        o_intra = jnp.einsum('bhts,bhse->bhte', A, ib)
        G_last = G[:, :, -1:, :]
        S_new = jnp.exp(G_last[:, :, 0, :, None]) * S + jnp.einsum(
            'bhsd,bhse->bhde', kb * jnp.exp(G_last - G), ib)
        return S_new, o_inter + o_intra

    S0 = jnp.zeros((B, HG_HEADS, HG_HEAD_DIM, HG_HEAD_DIM), f32)
    _, o = lax.scan(step, S0, (qc, kc, ic, gc))
    o = o.transpose(1, 0, 3, 2, 4).reshape(B, -1, HG_HEADS, HG_HEAD_DIM)[:, pad:]
    o = o * lax.rsqrt(jnp.mean(jnp.square(o), axis=-1, keepdims=True) + RMS_EPS) * g_norm
    o = o * jax.nn.silu(gz.astype(f32).reshape(B, L, HG_HEADS, HG_HEAD_DIM))
    return o.reshape(B, L, HG_WIDTH).astype(q.dtype)


def hybrid_layer(x, lb, w_in, w_conv, w_pool, pool_scale, hg_norm_g, w_o, ln1_g, ln1_b,
                 w_up, w_ffn_conv, b_ffn_conv, w_down, ln2_g, ln2_b):
    h = x @ w_in
    cb, cc, cv, hq, hf, hi, hgate, pv = jnp.split(h, SPLIT_IDX, axis=-1)
    y_conv = short_conv_mixer(cb, cc, cv, w_conv)
    y_hg = hgrn2_mixer(hq, hf, hi, hgate, lb, hg_norm_g)
    y_pool = multiscale_pool_mixer(pv, w_pool, pool_scale)
    mix = jnp.concatenate([y_conv, y_hg, y_pool], axis=-1) @ w_o
    x = layer_norm(ALPHA * x + mix, ln1_g, ln1_b)
    u = causal_dwconv(x @ w_up, w_ffn_conv, b_ffn_conv)
    gate, val = jnp.split(u, 2, axis=-1)
    ffn = (jax.nn.silu(gate) * val) @ w_down
    return layer_norm(ALPHA * x + ffn, ln2_g, ln2_b)


def setup_inputs(seed: int = 0) -> dict:
    key = jax.random.key(seed)
    ks = jax.random.split(key, 17)
    f32 = jnp.float32

    def nrm(k, shape):
        return jax.random.normal(k, shape, f32)

    col_scale = jnp.concatenate([
        jnp.ones((2 * CONV_WIDTH,), f32), jnp.full((CONV_WIDTH,), BETA, f32),
        jnp.ones((2 * HG_WIDTH,), f32), jnp.full((HG_WIDTH,), BETA, f32),
        jnp.ones((HG_WIDTH,), f32), jnp.full((POOL_WIDTH,), BETA, f32)])
    return {
        'x': nrm(ks[0], (BATCH, SEQ, D_MODEL)),
        'meta_tokens': nrm(ks[1], (N_META, D_MODEL)),
        'hg_lower_bounds': 0.1 * nrm(ks[2], (DEPTH, HG_WIDTH)),
        'w_in': nrm(ks[3], (DEPTH, D_MODEL, D_IN)) * (D_MODEL ** -0.5) * col_scale,
        'w_conv': nrm(ks[4], (DEPTH, CONV_WIDTH, SHORT_CONV_K)) * (SHORT_CONV_K ** -0.5),
        'w_pool': nrm(ks[5], (DEPTH, POOL_GROUPS, POOL_GROUP_DIM, POOL_GROUP_DIM)) * (POOL_GROUP_DIM ** -0.5),
        'pool_scale': 1.0 + 0.02 * nrm(ks[6], (DEPTH, POOL_WIDTH)),
        'hg_norm_g': 1.0 + 0.02 * nrm(ks[7], (DEPTH, HG_HEAD_DIM)),
        'w_o': nrm(ks[8], (DEPTH, D_MIX, D_MODEL)) * (D_MIX ** -0.5) * BETA,
        'ln1_g': 1.0 + 0.02 * nrm(ks[9], (DEPTH, D_MODEL)),
        'ln1_b': 0.02 * nrm(ks[10], (DEPTH, D_MODEL)),
        'w_up': nrm(ks[11], (DEPTH, D_MODEL, 2 * D_FF)) * (D_MODEL ** -0.5),
        'w_ffn_conv': nrm(ks[12], (DEPTH, 2 * D_FF, FFN_CONV_K)) * (FFN_CONV_K ** -0.5),
        'b_ffn_conv': 0.02 * nrm(ks[13], (DEPTH, 2 * D_FF)),
        'w_down': nrm(ks[14], (DEPTH, D_FF, D_MODEL)) * (D_FF ** -0.5) * BETA,
        'ln2_g': 1.0 + 0.02 * nrm(ks[15], (DEPTH, D_MODEL)),
        'ln2_b': 0.02 * nrm(ks[16], (DEPTH, D_MODEL)),
    }


def reference(x, meta_tokens, hg_lower_bounds, w_in, w_conv, w_pool, pool_scale, hg_norm_g,
              w_o, ln1_g, ln1_b, w_up, w_ffn_conv, b_ffn_conv, w_down, ln2_g, ln2_b):
    B = x.shape[0]
    meta = jnp.broadcast_to(meta_tokens[None].astype(x.dtype), (B, N_META, D_MODEL))
    h = jnp.concatenate([meta, x], axis=1)
    p = jax.nn.softmax(hg_lower_bounds.astype(jnp.float32), axis=0)
    lbs = jnp.cumsum(p, axis=0) - p[0]
    for l in range(DEPTH):
        h = hybrid_layer(h, lbs[l], w_in[l], w_conv[l], w_pool[l], pool_scale[l], hg_norm_g[l],
                         w_o[l], ln1_g[l], ln1_b[l], w_up[l], w_ffn_conv[l], b_ffn_conv[l],
                         w_down[l], ln2_g[l], ln2_b[l])
    return h[:, N_META:]
```

```python
import contextlib
import numpy as np
import concourse.bass as bass
import concourse.mybir as mybir
from concourse.bass_utils import run_bass_kernel_spmd

F32 = mybir.dt.float32
BF16 = mybir.dt.bfloat16
AF = mybir.ActivationFunctionType
ALU = mybir.AluOpType
AX = mybir.AxisListType

D = 1024
NT = 2176
NTILE = 17
OWN = 2048
HALO = 128
DFF = 2816
DIN = 3072
NPC = 22
ALPHA = float(4.0 ** 0.25)
LN_EPS = 1e-5
RMS_EPS = 1e-6
C_CB, C_CC, C_CV, C_Q, C_F, C_I, C_G, C_PV = 0, 256, 512, 768, 1280, 1792, 2304, 2816
FFN_GROUPS = [(0, 6), (6, 11), (11, 17), (17, 22)]
NPP = 19 + 132 + 44
WINS = [(0, 512), (512, 512), (1024, 512), (1536, 512), (2048, 128)]
NDMA = 24


class Sched:
    def __init__(self, nc, es):
        self.nc = nc
        self.eng = {'pe': nc.tensor, 'act': nc.scalar, 'dve': nc.vector,
                    'pool': nc.gpsimd, 'sp': nc.sync}
        self.semh = {}
        for e in ['pe', 'act', 'dve', 'pool']:
            self.semh[e] = es.enter_context(nc.semaphore("sem_" + e))
        for j in range(NDMA):
            self.semh[('dma', j)] = es.enter_context(nc.semaphore("semd%d" % j))
        self.semh['cc'] = es.enter_context(nc.semaphore("sem_cc"))
        self.cnt = {e: 0 for e in ['pe', 'act', 'dve', 'pool', 'cc']}
        self.known = {e: {} for e in self.eng}
        self.lastw = {}
        self.readers = {}
        self.dma_use = [0] * NDMA
        self.dma_nextq = [0, 0]
        self.latest = {}
        self.cap = None

    def _wait(self, e, tok):
        if tok is None:
            return
        sk, v = tok
        if e == 'pe' and sk == 'pe':
            return
        if self.known[e].get(sk, 0) >= v:
            return
        self.eng[e].wait_ge(self.semh[sk], v)
        self.known[e][sk] = v

    def deps(self, e, reads, writes):
        for k in reads:
            self._wait(e, self.lastw.get(k))
        for k in writes:
            self._wait(e, self.lastw.get(k))
            for sk, v in self.readers.get(k, {}).items():
                self._wait(e, (sk, v))

    def commit(self, tok, reads, writes):
        sk, v = tok
        self.latest[sk] = max(self.latest.get(sk, 0), v)
        for k in reads:
            r = self.readers.setdefault(k, {})
            r[sk] = max(r.get(sk, 0), v)
        for k in writes:
            self.lastw[k] = tok
            self.readers[k] = {}

    def capture_ops(self, body):
        self.cap = []
        body()
        out, self.cap = self.cap, None
        return out

    def op(self, e, reads, writes, fn):
        if self.cap is not None:
            self.cap.append(lambda: self._op(e, reads, writes, fn))
            return
        self._op(e, reads, writes, fn)

    def _op(self, e, reads, writes, fn):
        self.deps(e, reads, writes)
        ins = fn(self.eng[e])
        self.cnt[e] += 1
        ins.then_inc(self.semh[e], 1)
        self.commit((e, self.cnt[e]), reads, writes)

    def dma(self, q, reads, writes, fn):
        if self.cap is not None:
            self.cap.append(lambda: self._dma(q, reads, writes, fn))
            return
        self._dma(q, reads, writes, fn)

    def _dma(self, q, reads, writes, fn):
        self.deps(q, reads, writes)
        half = NDMA // 2
        qi = 0 if q == 'sp' else 1
        j = qi * half + self.dma_nextq[qi]
        self.dma_nextq[qi] = (self.dma_nextq[qi] + 1) % half
        r = self.dma_use[j]
        if r > 0:
            self._wait(q, (('dma', j), 16 * r))
        ins = fn(self.eng[q])
        ins.then_inc(self.semh[('dma', j)], 16)
        self.dma_use[j] += 1
        self.commit((('dma', j), 16 * (r + 1)), reads, writes)

    def cc(self, q, reads, writes, fn):
        self.deps(q, reads, writes)
        ins = fn(self.eng[q])
        self.cnt['cc'] += 1
        ins.then_inc(self.semh['cc'])
        self.commit(('cc', self.cnt['cc']), reads, writes)

    def barrier(self, engines=('pe', 'act', 'dve', 'pool', 'sp')):
        for e in engines:
            for sk, v in list(self.latest.items()):
                self._wait(e, (sk, v))


def build_program(mode='fused', stop=None):
    nc = bass.Bass("TRN2", target_bir_lowering=False)
    es = contextlib.ExitStack()

    def din(name, shape, dt=F32):
        return nc.dram_tensor(name, list(shape), dt, kind="ExternalInput").ap()

    def dout(name, shape, dt=F32):
        return nc.dram_tensor(name, list(shape), dt, kind="ExternalOutput").ap()

    xw = din("xw", [NT, D])
    consts_d = din("consts", [128, 640])
    rc_d = din("rc", [128, 32])
    mask0_d = din("mask0", [128, 1])
    sel_d = din("sel", [128, 4])
    WD = []
    for l in range(2):
        WD.append(dict(
            w_in=din("w_in%d" % l, [D, DIN]), pp=din("pp%d" % l, [128, NPP]),
            w_o=din("w_o%d" % l, [D, D]), w_up=din("w_up%d" % l, [D, 2 * DFF]),
            w_down=din("w_down%d" % l, [DFF, D]), w_pool=din("w_pool%d" % l, [4, 64, 64]),
            lnp=din("lnp%d" % l, [4, 128, D]),
            xin=nc.dram_tensor("xchg_in%d" % l, [128, 516], F32),
            xout=nc.dram_tensor("xchg_out%d" % l, [512, 516], F32)))
    yw = dout("yw", [OWN, D])

    with es:
        S = Sched(nc, es)

        sfx = [""]

        uid = [0]

        def sb(name, shape, dt=F32, stack=es):
            uid[0] += 1
            return stack.enter_context(nc.sbuf_tensor("%s%s_%d" % (name, sfx[0], uid[0]), list(shape), dt))

        x_res = sb("x_res", [128, NTILE, D])
        xT = sb("xT", [128, 8, NT], BF16)
        pp = sb("pp_sb", [128, NPP])
        cst = sb("cst", [128, 640])
        ident = sb("ident", [128, 128], BF16)
        onesb = sb("onesb", [128, 128], BF16)
        epsb = sb("epsb", [128, 3])
        mscan = sb("mscan", [128, 512])
        lbt = sb("lbt", [128, 8])
        noml = sb("noml", [128, 4])
        lb2 = sb("lb2", [128, 12])
        gn2 = sb("gn2", [128, 1])
        zbf = sb("zbf", [128, 128], BF16)
        psb = [es.enter_context(nc.psum_tensor("ps%d" % i, [128, 512], F32)) for i in range(7)]
        pst = es.enter_context(nc.psum_tensor("pst", [128, 1024], BF16))
        mask4 = cst[:, 128:640]
        mask0 = sb("mask0_sb", [128, 1])
        if True:
            S.dma('sp', [], ['mask0'], lambda q: q.dma_start(out=mask0[:, :], in_=mask0_d[:, :]))

        S.dma('sp', [], ['cst'], lambda q: q.dma_start(out=cst[:, :], in_=consts_d[:, :]))
        S.op('dve', ['cst'], ['ident'], lambda v: v.tensor_copy(out=ident[:, :], in_=cst[:, 0:128]))
        S.op('dve', [], ['onesb'], lambda v: v.memset(onesb[:, :], 1.0))
        S.op('dve', [], ['epsb'], lambda v: v.memset(epsb[:, 0:1], float(128.0 * RMS_EPS)))
        S.op('dve', ['epsb'], ['epsb'], lambda v: v.memset(epsb[:, 1:2], float(LN_EPS)))
        S.op('dve', ['epsb'], ['epsb'], lambda v: v.memset(epsb[:, 2:3], 1.0))
        S.op('dve', [], ['zbf'], lambda v: v.memset(zbf[:, :], 0.0))
        S.op('dve', [], ['mscan'], lambda v: v.memset(mscan[:, :], 1.0))
        S.op('dve', ['mscan'], ['mscan'], lambda v: v.memset(
            mscan[:, :].rearrange("p (c s) -> p c s", s=64)[:, :, 0:1], 0.0))
        def tile_to_xT(j, src_bf, src_key):
            for k in range(8):
                S.op('pe', [src_key, 'ident'], [('pst',)], lambda t, k=k: t.transpose(
                    out=pst[:, k * 128:(k + 1) * 128], in_=src_bf[:, k * 128:(k + 1) * 128],
                    identity=ident[:, :]))
            S.op('act', [('pst',)], [('xT', j)], lambda a: a.activation(
                out=xT[:, :, j * 128:(j + 1) * 128],
                in_=pst[:, :].rearrange("p (k t) -> p k t", t=128), func=AF.Copy))

        def wblock_load(dst, dst_key, src, col0, ncols, dcol0):
            S.dma('pool', [], [dst_key], lambda q: q.dma_start(
                out=dst[:, :, dcol0:dcol0 + ncols],
                in_=src.rearrange("(k p) n -> p k n", p=128)[:, :, col0:col0 + ncols]))

        def inproj_fm(ps, ps_key, wt, wt_key, wcol, t0, W):
            xkeys = [('xT', j) for j in range(t0 // 128, (t0 + W + 127) // 128)]
            for k in range(8):
                S.op('pe', [wt_key] + xkeys, [ps_key], lambda t, k=k: t.matmul(
                    ps[:, 0:W], lhsT=wt[:, k, wcol:wcol + 128], rhs=xT[:, k, t0:t0 + W],
                    start=(k == 0), stop=(k == 7)))

        def finish():
            for j in range(1, NTILE):
                S.dma('sp', [('xres', j)], [], lambda q, j=j: q.dma_start(
                    out=yw[(j - 1) * 128:j * 128, :], in_=x_res[:, j, :]))
            S.barrier()
            return nc

        with contextlib.ExitStack() as ph:
            xb = [sb("xb%d" % i, [128, D], BF16, ph) for i in range(2)]
            for j in range(NTILE):
                S.dma('sp', [], [('xres', j)], lambda q, j=j: q.dma_start(
                    out=x_res[:, j, :], in_=xw[j * 128:(j + 1) * 128, :]))
            for j in range(NTILE):
                b = xb[j % 2]
                S.op('dve', [('xres', j)], [('xb', j % 2)], lambda v, j=j, b=b: v.tensor_copy(
                    out=b[:, :], in_=x_res[:, j, :]))
                tile_to_xT(j, b, ('xb', j % 2))
            S.barrier()

        def emit_layer(layer):
            sfx[0] = '_L%d' % layer
            Wl = WD[layer]
            w_in, w_o, w_up, w_down, w_pool, lnp = (Wl[k] for k in ('w_in', 'w_o', 'w_up', 'w_down', 'w_pool', 'lnp'))
            xin, xout = Wl['xin'].ap(), Wl['xout'].ap()
            S.dma('sp', [], ['pp'], lambda q: q.dma_start(out=pp[:, :], in_=Wl['pp'][:, :]))
            if layer == 0:
                S.op('dve', [], ['lbt'], lambda v: v.memset(lbt[:, 0:4], 0.0))
                S.op('dve', ['lbt'], ['lbt'], lambda v: v.memset(lbt[:, 4:8], 1.0))
            else:
                S.op('dve', ['pp'], ['lbt'], lambda v: v.tensor_tensor(
                    out=lbt[:, 4:8], in0=pp[:, 14:18], in1=pp[:, 10:14], op=ALU.subtract))
                S.op('act', ['lbt'], ['lbt'], lambda a: a.activation(
                    out=lbt[:, 0:4], in_=lbt[:, 4:8], func=AF.Sigmoid))
                S.op('dve', ['lbt'], ['lbt'], lambda v: v.tensor_scalar(
                    out=lbt[:, 4:8], in0=lbt[:, 0:4], scalar1=-1.0, scalar2=1.0,
                    op0=ALU.mult, op1=ALU.add))
            S.op('dve', ['lbt'], ['noml'], lambda v: v.tensor_scalar(
                out=noml[:, :], in0=lbt[:, 4:8], scalar1=-1.0, scalar2=None, op0=ALU.mult))
            S.op('dve', ['lbt'], ['lbt'], lambda v: v.tensor_scalar(
                out=lb2[:, 0:4], in0=lbt[:, 4:8], scalar1=0.5, scalar2=None, op0=ALU.mult))
            S.op('dve', ['lbt'], ['lbt'], lambda v: v.tensor_tensor(
                out=lb2[:, 4:8], in0=lb2[:, 0:4], in1=lbt[:, 0:4], op=ALU.add))
            S.op('dve', ['lbt'], ['lbt'], lambda v: v.tensor_scalar(
                out=lb2[:, 8:12], in0=lbt[:, 4:8], scalar1=-0.5, scalar2=None, op0=ALU.mult))
            S.op('dve', ['pp'], ['gn2'], lambda v: v.tensor_scalar(
                out=gn2[:, :], in0=pp[:, 18:19], scalar1=float(128.0 ** 0.5), scalar2=None,
                op0=ALU.mult))

            def hgrn2(ph, full, mixT, s_in, wb=None, preloaded=()):
                if wb is None:
                    wb = [sb("hwb%d" % i, [128, 8, 512], BF16, ph) for i in range(2)]
                names = ['sig', 'f', 'G', 'E2'] + (['q', 'E1', 'o', 'rs', 'o2'] if full else [])
                T = {n: sb("hT_" + n, [128, 512], F32, ph) for n in names}
                Bi2 = [sb("hBi%d" % i, [128, 512], BF16, ph) for i in range(2)]
                Bkh2 = [sb("hBkh%d" % i, [128, 512], BF16, ph) for i in range(2)]
                dec2 = [sb("hdec%d" % i, [128, 8], F32, ph) for i in range(2)]
                Bq2 = Bk2 = [None, None]
                tok4 = sb("htok4", [128, 4, 256], BF16, ph)
                Sf = [sb("hSf%d" % i, [128, 128], F32, ph) for i in range(2)]
                NSB = 12
                Sb = sb("hSb", [128, NSB, 128], BF16, ph)
                gsum = sb("hgsum", [128, 4], F32, ph)
                gtmp = sb("hgtmp", [128, 1], F32, ph)
                if full:
                    Tsg = [sb("hTsg%d" % i, [128, 512], F32, ph) for i in range(2)]
                    Bq2 = [sb("hBq%d" % i, [128, 512], BF16, ph) for i in range(2)]
                    Bk2 = [sb("hBk%d" % i, [128, 512], BF16, ph) for i in range(2)]
                    Am4 = sb("hAm4", [128, 512], BF16, ph)
                    o2h = sb("ho2h", [128, 512], BF16, ph)
                    o2l = sb("ho2l", [128, 512], BF16, ph)
                n0 = layer
                S.op('dve', [], ['gsum'], lambda v: v.memset(gsum[:, :], 0.0))

                def load_head(h):
                    w = wb[h % 2]
                    blocks = [(C_F, 128, 1), (C_I, 256, 2)]
                    if full:
                        blocks = [(C_Q, 0, 0), (C_G, 384, 3)] + blocks
                    for (cbase, dcol, bi) in blocks:
                        wblock_load(w, ('hwb', h % 2, bi), w_in, cbase + h * 128, 128, dcol)

                items = []
                for h in range(4):
                    for wi, (t0, W) in enumerate(WINS):
                        chunks = [8 * wi + c for c in range(W // 64)]
                        need_state = [n for n in chunks if n0 <= n < n0 + 32]
                        if full or need_state:
                            items.append((h, wi, t0, W, need_state))

                def inproj(it):
                    h, wi, t0, W, need_state = it
                    w = wb[h % 2]
                    inproj_fm(psb[1], ('ps', 1), w, ('hwb', h % 2, 1), 128, t0, W)
                    inproj_fm(psb[2], ('ps', 2), w, ('hwb', h % 2, 2), 256, t0, W)
                    if full:
                        inproj_fm(psb[0], ('ps', 0), w, ('hwb', h % 2, 0), 0, t0, W)
                        inproj_fm(psb[3], ('ps', 3), w, ('hwb', h % 2, 3), 384, t0, W)

                def pre_stage(it, par, part='ab'):
                    h, wi, t0, W, need_state = it
                    nch = W // 64
                    Bq, Bk, Bi, Bkh, dec = Bq2[par], Bk2[par], Bi2[par], Bkh2[par], dec2[par]
                    if 'a' in part:
                        if full:
                            S.op('act', [('ps', 0)], ['hq'], lambda a: a.activation(
                                out=T['q'][:, 0:W], in_=psb[0][:, 0:W], func=AF.Copy,
                                scale=float(128.0 ** -0.5)))
                        if full:
                            S.op('act', [('ps', 1)], ['hsig'], lambda a: a.activation(
                                out=T['sig'][:, 0:W], in_=psb[1][:, 0:W], func=AF.Tanh, scale=0.5))
                        else:
                            S.op('act', [('ps', 1)], ['hsig'], lambda a: a.activation(
                                out=T['sig'][:, 0:W], in_=psb[1][:, 0:W], func=AF.Exp, scale=-1.0))
                        S.op('act', [('ps', 2)], [('hBi', par)], lambda a: a.activation(
                            out=Bi[:, 0:W], in_=psb[2][:, 0:W], func=AF.Copy))
                        if full:
                            S.op('act', [('ps', 3)], [('hsg', par)], lambda a: a.activation(
                                out=Tsg[par][:, 0:W], in_=psb[3][:, 0:W], func=AF.Silu))
                    if 'b' not in part:
                        return
                    if full:
                        S.op('dve', ['hsig', 'lbt'], ['hf'], lambda v: v.tensor_scalar(
                            out=T['f'][:, 0:W], in0=T['sig'][:, 0:W], scalar1=lb2[:, h:h + 1],
                            scalar2=lb2[:, 4 + h:5 + h], op0=ALU.mult, op1=ALU.add))
                        S.op('dve', ['hf'], ['hf'], lambda v: v.tensor_scalar(
                            out=T['f'][:, 0:W], in0=T['f'][:, 0:W], scalar1=1e-30, scalar2=None,
                            op0=ALU.max))
                        S.op('dve', ['hsig', 'lbt'], ['hsig'], lambda v: v.tensor_scalar(
                            out=T['sig'][:, 0:W], in0=T['sig'][:, 0:W], scalar1=-1.0,
                            scalar2=lb2[:, 8 + h:9 + h], op0=ALU.add, op1=ALU.mult))
                        S.op('act', ['hf'], ['hf'], lambda a: a.activation(
                            out=T['f'][:, 0:W], in_=T['f'][:, 0:W], func=AF.Ln))
                    else:
                        S.op('act', ['hsig', 'lbt', 'epsb'], ['hf'], lambda a: a.activation(
                            out=T['f'][:, 0:W], in_=T['sig'][:, 0:W], func=AF.Ln,
                            scale=lbt[:, h:h + 1], bias=epsb[:, 2:3]))
                        S.op('act', ['hsig', 'epsb'], ['hG'], lambda a: a.activation(
                            out=T['G'][:, 0:W], in_=T['sig'][:, 0:W], func=AF.Ln, bias=epsb[:, 2:3]))
                        S.op('dve', ['hf', 'hG'], ['hf'], lambda v: v.tensor_tensor(
                            out=T['f'][:, 0:W], in0=T['f'][:, 0:W], in1=T['G'][:, 0:W], op=ALU.subtract))
                        S.op('dve', ['hf'], ['hf'], lambda v: v.tensor_scalar(
                            out=T['f'][:, 0:W], in0=T['f'][:, 0:W], scalar1=-69.07755279, scalar2=None,
                            op0=ALU.max))
                        S.op('act', ['hG'], ['hE2'], lambda a: a.activation(
                            out=T['E2'][:, 0:W], in_=T['G'][:, 0:W], func=AF.Exp, scale=-1.0))
                        S.op('dve', ['hsig', 'lbt', 'hE2'], ['hsig'], lambda v: v.scalar_tensor_tensor(
                            out=T['sig'][:, 0:W], in0=T['sig'][:, 0:W], scalar=lbt[:, 4 + h:5 + h],
                            in1=T['E2'][:, 0:W], op0=ALU.mult, op1=ALU.mult))
                    S.op('dve', ['hf', 'mscan'], ['hG'], lambda v: v.tensor_tensor_scan(
                        out=T['G'][:, 0:W], data0=mscan[:, 0:W], data1=T['f'][:, 0:W],
                        initial=0.0, op0=ALU.mult, op1=ALU.add))
                    Glast = T['G'][:, 0:W].rearrange("p (c s) -> p c s", s=64)[:, :, 63:64]
                    S.op('act', ['hG'], ['hE2'], lambda a: a.activation(
                        out=T['E2'][:, 0:W], in_=T['G'][:, 0:W], func=AF.Exp, scale=-1.0))
                    S.op('act', ['hG'], [('hdec', par)], lambda a: a.activation(
                        out=dec[:, 0:nch].rearrange("p (c o) -> p c o", o=1), in_=Glast,
                        func=AF.Exp))
                    if full:
                        S.op('act', ['hG'], ['hE1'], lambda a: a.activation(
                            out=T['E1'][:, 0:W], in_=T['G'][:, 0:W], func=AF.Exp))
                        S.op('dve', ['hq', 'hE1'], [('hBq', par)], lambda v: v.tensor_tensor(
                            out=Bq[:, 0:W], in0=T['q'][:, 0:W], in1=T['E1'][:, 0:W], op=ALU.mult))
                    S.op('dve', ['hsig', 'hE2'], ['hE2'], lambda v: v.tensor_tensor(
                        out=T['E2'][:, 0:W], in0=T['sig'][:, 0:W], in1=T['E2'][:, 0:W], op=ALU.mult))
                    if full:
                        S.op('dve', ['hE2'], [('hBk', par)], lambda v: v.tensor_copy(
                            out=Bk[:, 0:W], in_=T['E2'][:, 0:W]))
                    S.op('dve', ['hE2', ('hdec', par)], [('hBkh', par)], lambda v: v.tensor_tensor(
                        out=Bkh[:, 0:W].rearrange("p (c s) -> p c s", s=64),
                        in0=T['E2'][:, 0:W].rearrange("p (c s) -> p c s", s=64),
                        in1=dec[:, 0:nch].unsqueeze(2).to_broadcast([128, nch, 64]), op=ALU.mult))
                    if not full and need_state:
                        c0 = need_state[0] - 8 * wi
                        c1 = need_state[-1] - 8 * wi + 1
                        S.op('dve', ['hG'], ['gtmp'], lambda v: v.tensor_reduce(
                            out=gtmp[:, :], in_=Glast[:, c0:c1, :], axis=AX.XY, op=ALU.add))
                        S.op('dve', ['gtmp', 'gsum'], ['gsum'], lambda v: v.tensor_tensor(
                            out=gsum[:, h:h + 1], in0=gsum[:, h:h + 1], in1=gtmp[:, :], op=ALU.add))

                def o_evac(it, par):
                    h, wi, t0, W, need_state = it
                    S.op('act', [('ps', 5)], ['ho'], lambda a: a.activation(
                        out=T['o'][:, 0:W], in_=psb[5][:, 0:W], func=AF.Copy))
                    S.op('act', [('ps', 5)], ['ho2'], lambda a: a.activation(
                        out=T['o2'][:, 0:W], in_=psb[5][:, 0:W], func=AF.Square))
                    S.op('dve', ['ho2'], ['ho2h'], lambda v: v.tensor_copy(out=o2h[:, 0:W], in_=T['o2'][:, 0:W]))
                    S.op('dve', ['ho2', 'ho2h'], ['ho2l'], lambda v: v.tensor_tensor(
                        out=o2l[:, 0:W], in0=T['o2'][:, 0:W], in1=o2h[:, 0:W], op=ALU.subtract))
                    S.op('pe', ['ho2h', 'onesb'], [('ps', 6)], lambda t: t.matmul(
                        psb[6][:, 0:W], lhsT=onesb[:, :], rhs=o2h[:, 0:W], start=True, stop=False))
                    S.op('pe', ['ho2l', 'onesb'], [('ps', 6)], lambda t: t.matmul(
                        psb[6][:, 0:W], lhsT=onesb[:, :], rhs=o2l[:, 0:W], start=False, stop=True))
                    S.op('act', [('ps', 6), 'epsb'], ['hrs'], lambda a: a.activation(
                        out=T['rs'][:, 0:W], in_=psb[6][:, 0:W], func=AF.Ln, bias=epsb[:, 0:1]))
                    S.op('act', ['hrs'], ['hrs'], lambda a: a.activation(
                        out=T['rs'][:, 0:W], in_=T['rs'][:, 0:W], func=AF.Exp, scale=-0.5))
                    S.op('dve', ['ho', 'hrs'], ['ho'], lambda v: v.tensor_tensor(
                        out=T['o'][:, 0:W], in0=T['o'][:, 0:W], in1=T['rs'][:, 0:W], op=ALU.mult))
                    mk = [('mixT', jj) for jj in range(t0 // 128, (t0 + W) // 128)]
                    S.op('dve', ['ho', ('hsg', par), 'gn2'], mk, lambda v: v.scalar_tensor_tensor(
                        out=mixT[:, 2 + h, t0:t0 + W], in0=T['o'][:, 0:W], scalar=gn2[:, 0:1],
                        in1=Tsg[par][:, 0:W], op0=ALU.mult, op1=ALU.mult))

                st = dict(sfi=0, sbi=0, valid=False, h=-1)

                def tile_stage(it, par):
                    h, wi, t0, W, need_state = it
                    nch, ntl = W // 64, W // 128
                    Bq, Bk, Bi, Bkh = Bq2[par], Bk2[par], Bi2[par], Bkh2[par]
                    for j in range(ntl):
                        cs = slice(j * 128, (j + 1) * 128)
                        S.op('pe', [('hBi', par), 'ident'], [('pst',)], lambda t, cs=cs, j=j: t.transpose(
                            out=pst[:, j * 256:j * 256 + 128], in_=Bi[:, cs], identity=ident[:, :]))
                        S.op('pe', [('hBkh', par), 'ident'], [('pst',)], lambda t, cs=cs, j=j: t.transpose(
                            out=pst[:, j * 256 + 128:(j + 1) * 256], in_=Bkh[:, cs], identity=ident[:, :]))
                    S.op('act', [('pst',)], ['htok'], lambda a: a.activation(
                        out=tok4[:, 0:ntl, :], in_=pst[:, 0:ntl * 256].rearrange("p (j c) -> p j c", c=256),
                        func=AF.Copy))
                    if full:
                        for j in range(ntl):
                            cs = slice(j * 128, (j + 1) * 128)
                            S.op('pe', [('hBk', par), ('hBq', par)], [('ps', 4)], lambda t, cs=cs: t.matmul(
                                psb[4][:, cs], lhsT=Bk[:, cs], rhs=Bq[:, cs], start=True, stop=True))
                        S.op('dve', [('ps', 4), 'cst'], ['hAm'], lambda v: v.tensor_tensor(
                            out=Am4[:, 0:W], in0=psb[4][:, 0:W], in1=mask4[:, 0:W], op=ALU.mult))
                    upd = []
                    for c in range(nch):
                        n = 8 * wi + c
                        if n < n0:
                            continue
                        if not full and not (n0 <= n < n0 + 32):
                            continue
                        upd.append(c)
                    for c in upd:
                        j, c2 = divmod(c, 2)
                        rows = slice(c2 * 64, (c2 + 1) * 64)
                        bank = 5 + c % 2
                        S.op('pe', ['htok'], [('ps', bank)], lambda t, c=c, j=j, rows=rows, bank=bank: t.matmul(
                            psb[bank][:, (c // 2) * 128:(c // 2 + 1) * 128], lhsT=tok4[rows, j, 128:256],
                            rhs=tok4[rows, j, 0:128], start=True, stop=True))
                    return upd

                def chain_stage(it, par, upd):
                    h, wi, t0, W, need_state = it
                    nch, ntl = W // 64, W // 128
                    Bq, dec = Bq2[par], dec2[par]
                    st_for_chunk = []
                    for c in range(nch):
                        n = 8 * wi + c
                        if n == n0:
                            sfi, sbi = st['sfi'], st['sbi']
                            if s_in is not None:
                                S.op('dve', [('sin', h)], [('hSf', sfi)], lambda v, sfi=sfi: v.tensor_copy(
                                    out=Sf[sfi][:, :], in_=s_in[:, h, :]))
                            else:
                                S.op('dve', [], [('hSf', sfi)], lambda v, sfi=sfi: v.memset(Sf[sfi][:, :], 0.0))
                            if full:
                                S.op('dve', [('hSf', sfi)], [('hSb', sbi)], lambda v, sfi=sfi, sbi=sbi: v.tensor_copy(
                                    out=Sb[:, sbi, :], in_=Sf[sfi][:, :]))
                            st['valid'] = True
                        if not st['valid']:
                            st_for_chunk.append(None)
                            continue
                        st_for_chunk.append(st['sbi'])
                        if c not in upd:
                            continue
                        sfi, sbi = st['sfi'], st['sbi']
                        nsf, nsb = 1 - sfi, (sbi + 1) % NSB
                        bank = 5 + c % 2
                        S.op('dve', [('hSf', sfi), ('hdec', par), ('ps', bank)], [('hSf', nsf)],
                             lambda v, c=c, sfi=sfi, nsf=nsf, bank=bank: v.scalar_tensor_tensor(
                                 out=Sf[nsf][:, :], in0=Sf[sfi][:, :], scalar=dec[:, c:c + 1],
                                 in1=psb[bank][:, (c // 2) * 128:(c // 2 + 1) * 128],
                                 op0=ALU.mult, op1=ALU.add))
                        if full:
                            S.op('dve', [('hSf', nsf)], [('hSb', nsb)], lambda v, nsf=nsf, nsb=nsb: v.tensor_copy(
                                out=Sb[:, nsb, :], in_=Sf[nsf][:, :]))
                        st['sfi'], st['sbi'] = nsf, nsb
                        if not full and n == n0 + 31:
                            S.dma('sp', [('hSf', nsf)], ['xin'], lambda q, nsf=nsf: q.dma_start(
                                out=xin[:, h * 128:(h + 1) * 128], in_=Sf[nsf][:, :]))
                    if not full:
                        return
                    for j in range(ntl):
                        cs = slice(j * 128, (j + 1) * 128)
                        S.op('pe', ['htok', 'hAm'], [('ps', 5)], lambda t, j=j, cs=cs: t.matmul(
                            psb[5][:, cs], lhsT=tok4[:, j, 0:128], rhs=Am4[:, cs], start=True, stop=False))
                        for c2 in range(2):
                            sidx = st_for_chunk[2 * j + c2]
                            lhs = zbf[:, :] if sidx is None else Sb[:, sidx, :]
                            lk = 'zbf' if sidx is None else ('hSb', sidx)
                            S.op('pe', [lk, ('hBq', par)], [('ps', 5)], lambda t, j=j, c2=c2, lhs=lhs: t.matmul(
                                psb[5][:, j * 128 + c2 * 64:j * 128 + (c2 + 1) * 64], lhsT=lhs,
                                rhs=Bq[:, j * 128 + c2 * 64:j * 128 + (c2 + 1) * 64],
                                start=False, stop=(c2 == 1)))

                loaded = set(preloaded)
                if 0 not in loaded:
                    load_head(0)
                    loaded.add(0)
                inproj(items[0])
                pre_stage(items[0], 0, 'a')
                if len(items) > 1:
                    if items[1][0] not in loaded:
                        load_head(items[1][0])
                        loaded.add(items[1][0])
                    inproj(items[1])
                pre_stage(items[0], 0, 'b')
                for ii, it in enumerate(items):
                    h = it[0]
                    par = ii % 2
                    if h != st['h']:
                        st.update(h=h, valid=False)
                        if h + 1 < 4 and (h + 1) not in loaded:
                            load_head(h + 1)
                            loaded.add(h + 1)
                    nxt = items[ii + 1] if ii + 1 < len(items) else None
                    nn = items[ii + 2] if ii + 2 < len(items) else None
                    upd = tile_stage(it, par)
                    if nxt is not None:
                        pre_stage(nxt, 1 - par, 'a')
                    if nn is not None:
                        if nn[0] not in loaded:
                            load_head(nn[0])
                            loaded.add(nn[0])
                        inproj(nn)
                    A = S.capture_ops(lambda: pre_stage(nxt, 1 - par, 'b')) if nxt is not None else []
                    B = S.capture_ops(lambda: o_evac(it, par)) if full else []
                    na_exp = 3 if full else 2

                    def run(lst, a, b):
                        for t_ in lst[a:b]:
                            t_()
                    run(A, 0, 4)
                    chain_stage(it, par, upd)
                    run(B, 0, 2)
                    run(A, 4, 5)
                    run(B, 2, 6)
                    run(A, 5, 5 + na_exp)
                    run(B, 6, 8)
                    run(A, 5 + na_exp, len(A))
                    run(B, 8, len(B))
                if not full:
                    S.dma('sp', ['gsum'], ['xin'], lambda q: q.dma_start(out=xin[:, 512:516], in_=gsum[:, :]))

            with contextlib.ExitStack() as ph:
                hgrn2(ph, False, None, None)
                S.barrier()
            S.cc('pool', ['xin'], ['xout'], lambda g: g.collective_compute(
                "AllGather", ALU.bypass, replica_groups=[[0, 1, 2, 3], [4, 5, 6, 7]],
                ins=[xin.opt()], outs=[xout.opt()]))

            with contextlib.ExitStack() as mph:
                mixT = sb("mixT", [128, 8, NT], BF16, mph)
                with contextlib.ExitStack() as ph:
                    cwb = [sb("cwb%d" % i, [128, 8, 384], BF16, ph) for i in range(2)]
                    pwb = [sb("pwb%d" % i, [128, 8, 128], BF16, ph) for i in range(2)]
                    for j in range(2):
                        for bi, cbase in enumerate((C_CB, C_CC, C_CV)):
                            wblock_load(cwb[j], ('cwb', j, bi), w_in, cbase + j * 128, 128, bi * 128)
                    for j in range(2):
                        wblock_load(pwb[j], ('pwb', j), w_in, C_PV + j * 128, 128, 0)
                    A = sb("cA", [128, 16 + NT], F32, ph)
                    Bb = sb("cB", [128, 16 + NT], F32, ph)
                    Cc = sb("cC", [128, 16 + NT], F32, ph)
                    cct = [sb("cct%d" % i, [128, 512], F32, ph) for i in range(2)]
                    dbf = sb("cdbf", [128, NT], BF16, ph)
                    wpd = sb("wpd", [128, 2, 128], BF16, ph)
                    rc = sb("rc_sb", [128, 32], F32, ph)
                    S.dma('sp', [], ['rc'], lambda q: q.dma_start(out=rc[:, :], in_=rc_d[:, :]))
                    S.op('dve', [], ['wpd'], lambda v: v.memset(wpd[:, :, :], 0.0))
                    for g in range(4):
                        r0 = (g % 2) * 64
                        S.dma('pool', [], ['wpd'], lambda q, g=g, r0=r0: q.dma_start(
                            out=wpd[r0:r0 + 64, g // 2, r0:r0 + 64], in_=w_pool[g, :, :]))
                    for buf, key in ((A, 'cA'), (Bb, 'cB'), (Cc, 'cC')):
                        S.op('pool', [], [key], lambda g, buf=buf: g.memset(buf[:, 0:16], 0.0))
                    allmix = [('mixT', jj) for jj in range(NTILE)]
                    for j in range(2):
                        w = cwb[j]
                        for wi, (t0, W) in enumerate(WINS):
                            o3 = (wi % 2) * 3
                            inproj_fm(psb[o3 + 0], ('ps', o3 + 0), w, ('cwb', j, 0), 0, t0, W)
                            inproj_fm(psb[o3 + 1], ('ps', o3 + 1), w, ('cwb', j, 1), 128, t0, W)
                            inproj_fm(psb[o3 + 2], ('ps', o3 + 2), w, ('cwb', j, 2), 256, t0, W)
                            ct = cct[wi % 2]
                            S.op('act', [('ps', o3 + 0)], ['cA'], lambda a, o3=o3: a.activation(
                                out=A[:, 16 + t0:16 + t0 + W], in_=psb[o3][:, 0:W], func=AF.Copy))
                            S.op('act', [('ps', o3 + 1)], [('cct', wi % 2)], lambda a, o3=o3, ct=ct: a.activation(
                                out=ct[:, 0:W], in_=psb[o3 + 1][:, 0:W], func=AF.Copy))
                            S.op('dve', [('cct', wi % 2), ('ps', o3 + 2)], ['cB'], lambda v, o3=o3, ct=ct: v.tensor_tensor(
                                out=Bb[:, 16 + t0:16 + t0 + W], in0=ct[:, 0:W], in1=psb[o3 + 2][:, 0:W],
                                op=ALU.mult))
                        wc = lambda kk: pp[:, j * 3 + kk:j * 3 + kk + 1]
                        S.op('dve', ['cB', 'pp'], ['cC'], lambda v: v.tensor_scalar(
                            out=Cc[:, 16:16 + NT], in0=Bb[:, 16:16 + NT], scalar1=wc(2), scalar2=None,
                            op0=ALU.mult))
                        S.op('dve', ['cB', 'cC', 'pp'], ['cC'], lambda v: v.scalar_tensor_tensor(
                            out=Cc[:, 16:16 + NT], in0=Bb[:, 15:15 + NT], scalar=wc(1),
                            in1=Cc[:, 16:16 + NT], op0=ALU.mult, op1=ALU.add))
                        S.op('dve', ['cB', 'cC', 'pp'], ['cC'], lambda v: v.scalar_tensor_tensor(
                            out=Cc[:, 16:16 + NT], in0=Bb[:, 14:14 + NT], scalar=wc(0),
                            in1=Cc[:, 16:16 + NT], op0=ALU.mult, op1=ALU.add))
                        S.op('dve', ['cA', 'cC'], allmix, lambda v: v.tensor_tensor(
                            out=mixT[:, j, :], in0=Cc[:, 16:16 + NT], in1=A[:, 16:16 + NT], op=ALU.mult))
                    for j in range(2):
                        w = pwb[j]
                        for wi, (t0, W) in enumerate(WINS):
                            pi = wi % 2
                            inproj_fm(psb[pi], ('ps', pi), w, ('pwb', j), 0, t0, W)
                            S.op('act', [('ps', pi)], ['cA'], lambda a, pi=pi: a.activation(
                                out=A[:, 16 + t0:16 + t0 + W], in_=psb[pi][:, 0:W], func=AF.Copy))
                        V = A
                        def wsum(dst, dk, src, sk, sh, p0=0):
                            S.op('pool', [sk, dk], [dk], lambda g: g.tensor_tensor(
                                out=dst[p0:128, 16:16 + NT], in0=src[p0:128, 16:16 + NT],
                                in1=src[p0:128, 16 - sh:16 - sh + NT], op=ALU.add))
                        wsum(Bb, 'cB', V, 'cA', 1)
                        if j == 0:
                            wsum(Cc, 'cC', Bb, 'cB', 2, 64)
                            lo, hi = Bb, Cc
                            lok, hik = 'cB', 'cC'
                        else:
                            wsum(Cc, 'cC', Bb, 'cB', 2)
                            wsum(Bb, 'cB', Cc, 'cC', 4)
                            wsum(Cc, 'cC', Bb, 'cB', 8, 64)
                            lo, hi = Bb, Cc
                            lok, hik = 'cB', 'cC'
                        for (p0, p1, src, sk) in ((0, 64, lo, lok), (64, 128, hi, hik)):
                            S.op('dve', [sk, 'cA', 'pp'], ['cdbf'], lambda v, p0=p0, p1=p1, src=src: v.scalar_tensor_tensor(
                                out=dbf[p0:p1, :], in0=src[p0:p1, 16:16 + NT], scalar=pp[p0:p1, 8 + j:9 + j],
                                in1=V[p0:p1, 16:16 + NT], op0=ALU.mult, op1=ALU.subtract))
                            S.op('dve', [sk, 'rc'], [('cct', 0)], lambda v, p0=p0, p1=p1, src=src: v.tensor_tensor(
                                out=cct[0][p0:p1, 0:16], in0=src[p0:p1, 16 + 112:16 + 128],
                                in1=rc[p0:p1, j * 16:(j + 1) * 16], op=ALU.mult))
                            S.op('dve', [('cct', 0), 'cA', 'cdbf'], ['cdbf'], lambda v, p0=p0, p1=p1: v.tensor_tensor(
                                out=dbf[p0:p1, 112:128], in0=cct[0][p0:p1, 0:16],
                                in1=V[p0:p1, 16 + 112:16 + 128], op=ALU.subtract))
                        for wi, (t0, W) in enumerate(WINS):
                            pi = 2 + wi % 2
                            S.op('pe', ['cdbf', 'wpd'], [('ps', pi)], lambda t, pi=pi: t.matmul(
                                psb[pi][:, 0:W], lhsT=wpd[:, j, :], rhs=dbf[:, t0:t0 + W], start=True, stop=True))
                            mk = [('mixT', jj) for jj in range(t0 // 128, (t0 + W) // 128)]
                            S.op('act', [('ps', pi), 'pp'], mk, lambda a, pi=pi: a.activation(
                                out=mixT[:, 6 + j, t0:t0 + W], in_=psb[pi][:, 0:W], func=AF.Copy,
                                scale=pp[:, 6 + j:7 + j]))
                    S.barrier()

                wbF = [sb("hwbF%d" % i, [128, 8, 512], BF16, mph) for i in range(2)]
                for hh in range(2):
                    for (cbase, dcol, bi) in ((C_Q, 0, 0), (C_G, 384, 3), (C_F, 128, 1), (C_I, 256, 2)):
                        wblock_load(wbF[hh], ('hwb', hh, bi), w_in, cbase + hh * 128, 128, dcol)
                with contextlib.ExitStack() as ph:
                    s_in = sb("s_in", [128, 4, 128], F32, mph)
                    sj = sb("sj", [128, 4, 128], F32, ph)
                    tt = sb("sjt", [128, 128], F32, ph)
                    dgl = sb("dgl", [128, 16], F32, ph)
                    dge = sb("dge", [128, 16], F32, ph)
                    sel = sb("sel_sb", [128, 4], F32, ph)
                    for r in range(4):
                        S.dma('sp', ['xout'], ['dgl'], lambda q, r=r: q.dma_start(
                            out=dgl[:, r * 4:(r + 1) * 4], in_=xout[r * 128:(r + 1) * 128, 512:516]))
                    S.dma('sp', [], ['sel'], lambda q: q.dma_start(out=sel[:, :], in_=sel_d[:, :]))
                    S.op('act', ['dgl'], ['dge'], lambda a: a.activation(
                        out=dge[:, :], in_=dgl[:, :], func=AF.Exp))
                    for h in range(4):
                        S.op('dve', [], [('sin', h)], lambda v, h=h: v.memset(s_in[:, h, :], 0.0))
                    sj3 = [sj, sb("sj1", [128, 4, 128], F32, ph), sb("sj2", [128, 4, 128], F32, ph)]
                    for r in range(3):
                        S.dma('sp', ['xout'], [('sj', r)], lambda q, r=r: q.dma_start(
                            out=sj3[r][:, :, :],
                            in_=xout[r * 128:(r + 1) * 128, 0:512].rearrange("d (h e) -> d h e", h=4)))
                    for r in range(3):
                        sj = sj3[r]
                        for h in range(4):
                            S.op('dve', [('sin', h), 'dge', ('sj', r)], ['sjt'], lambda v, h=h, r=r, sj=sj: v.scalar_tensor_tensor(
                                out=tt[:, :], in0=s_in[:, h, :], scalar=dge[:, r * 4 + h:r * 4 + h + 1],
                                in1=sj[:, h, :], op0=ALU.mult, op1=ALU.add))
                            S.op('dve', ['sjt', ('sin', h)], ['sjt'], lambda v, h=h: v.tensor_tensor(
                                out=tt[:, :], in0=tt[:, :], in1=s_in[:, h, :], op=ALU.subtract))
                            S.op('dve', ['sjt', 'sel', ('sin', h)], [('sin', h)], lambda v, h=h, r=r: v.scalar_tensor_tensor(
                                out=s_in[:, h, :], in0=tt[:, :], scalar=sel[:, r:r + 1],
                                in1=s_in[:, h, :], op0=ALU.mult, op1=ALU.add))
                    S.barrier()

                if stop == 'combine':
                    return finish()
                if stop == 'convpool':
                    return finish()
                with contextlib.ExitStack() as ph:
                    hgrn2(ph, True, mixT, s_in, wb=wbF, preloaded=(0, 1))
                    S.barrier()

                if stop == 'hg':
                    return finish()
                def layernorm_tiles(ph, gi, post):
                    gbc = sb("gbc%d" % gi, [128, D], F32, ph)
                    bbc = sb("bbc%d" % gi, [128, D], F32, ph)
                    S.dma('sp', [], ['gbc'], lambda q: q.dma_start(out=gbc[:, :], in_=lnp[gi, :, :]))
                    S.dma('sp', [], ['bbc'], lambda q: q.dma_start(out=bbc[:, :], in_=lnp[gi + 1, :, :]))
                    st = sb("lnst%d" % gi, [128, NTILE, 12], F32, ph)
                    mv = sb("lnmv%d" % gi, [128, NTILE, 2], F32, ph)
                    rs = sb("lnrs%d" % gi, [128, NTILE], F32, ph)
                    xn = [sb("lnxn%d_%d" % (gi, i), [128, D], F32, ph) for i in range(2)]
                    xb = [sb("lnxb%d_%d" % (gi, i), [128, D], BF16, ph) for i in range(2)]

                    def ln_stats(j):
                        for c in range(2):
                            S.op('dve', [('xres', j)], [('lnst', j)], lambda v, c=c: v.bn_stats(
                                out=st[:, j, c * 6:(c + 1) * 6], in_=x_res[:, j, c * 512:(c + 1) * 512]))
                        S.op('dve', [('lnst', j)], ['lnmv'], lambda v: v.bn_aggr(
                            out=mv[:, j, :], in_=st[:, j, :]))

                    def ln_apply(j0=0):
                        S.op('act', ['lnmv', 'epsb'], ['lnrs'], lambda a: a.activation(
                            out=rs[:, :].rearrange("p (t o) -> p t o", o=1), in_=mv[:, :, 1:2],
                            func=AF.Ln, bias=epsb[:, 1:2]))
                        S.op('act', ['lnrs'], ['lnrs'], lambda a: a.activation(
                            out=rs[:, :], in_=rs[:, :], func=AF.Exp, scale=-0.5))
                        for j in range(j0, NTILE):
                            b = j % 2
                            S.op('dve', [('xres', j), 'lnmv', 'gbc'], [('lnxn', b)], lambda v: v.scalar_tensor_tensor(
                                out=xn[b][:, :], in0=x_res[:, j, :], scalar=mv[:, j, 0:1], in1=gbc[:, :],
                                op0=ALU.subtract, op1=ALU.mult))
                            S.op('dve', [('lnxn', b), 'lnrs', 'bbc'], [('xres', j)], lambda v: v.scalar_tensor_tensor(
                                out=x_res[:, j, :], in0=xn[b][:, :], scalar=rs[:, j:j + 1], in1=bbc[:, :],
                                op0=ALU.mult, op1=ALU.add))
                            if j == 0:
                                S.op('dve', [('xres', 0), 'mask0'], [('xres', 0)], lambda v: v.tensor_scalar(
                                    out=x_res[:, 0, :], in0=x_res[:, 0, :], scalar1=mask0[:, 0:1], scalar2=None,
                                    op0=ALU.mult))

                        def cast(j):
                            S.op('act', [('xres', j)], [('lnxb', j % 2)], lambda a: a.activation(
                                out=xb[j % 2][:, :], in_=x_res[:, j, :], func=AF.Copy))

                        cast(j0)
                        for j in range(j0, NTILE):
                            src = xb[j % 2]
                            for k in range(8):
                                S.op('pe', [('lnxb', j % 2), 'ident'], [('pst',)], lambda t, k=k: t.transpose(
                                    out=pst[:, k * 128:(k + 1) * 128], in_=src[:, k * 128:(k + 1) * 128],
                                    identity=ident[:, :]))
                            if j + 1 < NTILE:
                                cast(j + 1)
                            S.op('act', [('pst',)], [('xT', j)], lambda a: a.activation(
                                out=xT[:, :, j * 128:(j + 1) * 128],
                                in_=pst[:, :].rearrange("p (k t) -> p k t", t=128), func=AF.Copy))
                    return ln_stats, ln_apply

                with contextlib.ExitStack() as ph:
                    wo = sb("wo", [128, 8, D], BF16, ph)
                    for half in range(2):
                        wblock_load(wo, 'wo', w_o, half * 512, 512, half * 512)
                    ln_stats, ln_apply = layernorm_tiles(ph, 0, None)
                    for j in range(NTILE):
                        for fb in range(2):
                            pi = (2 * j + fb) % 4
                            for k in range(8):
                                S.op('pe', [('mixT', j), 'wo'], [('ps', pi)], lambda t, k=k, pi=pi: t.matmul(
                                    psb[pi][:, :], lhsT=mixT[:, k, j * 128:(j + 1) * 128],
                                    rhs=wo[:, k, fb * 512:(fb + 1) * 512], start=(k == 0), stop=(k == 7)))
                            S.op('dve', [('xres', j), ('ps', pi)], [('xres', j)], lambda v, pi=pi: v.scalar_tensor_tensor(
                                out=x_res[:, j, fb * 512:(fb + 1) * 512], in0=x_res[:, j, fb * 512:(fb + 1) * 512],
                                scalar=ALPHA, in1=psb[pi][:, :], op0=ALU.mult, op1=ALU.add))
                        ln_stats(j)
                    ln_apply()
                    S.barrier()
            S.barrier()

            if stop == 'wo':
                return finish()
            FW = [(0, 512), (510, 512), (1020, 512), (1530, 512), (2040, 136)]
            fj0 = 0
            if layer == 1:
                FW = [(126, 412), (536, 412), (946, 412), (1356, 412), (1766, 410)]
                fj0 = 1
            with contextlib.ExitStack() as ph:
                gbuf = sb("gbuf", [128, 6, NT], BF16, ph)
                wdn = sb("wdn", [128, 6, D], BF16, ph)
                wup = [sb("wup%d" % i, [128, 8, 256], BF16, ph) for i in range(3)]
                ug = [sb("ug%d" % i, [128, 512], F32, ph) for i in range(2)]
                uv = [sb("uv%d" % i, [128, 512], F32, ph) for i in range(2)]
                sgt = [sb("sgt%d" % i, [128, 512], F32, ph) for i in range(2)]
                ln_stats2, ln_apply2 = layernorm_tiles(ph, 2, None)
                it = 0

                def load_wup(pc):
                    wblock_load(wup[pc % 3], ('wup', pc % 3, 0), w_up, pc * 128, 128, 0)
                    wblock_load(wup[pc % 3], ('wup', pc % 3, 1), w_up, DFF + pc * 128, 128, 128)

                for gi, (pc0, pc1) in enumerate(FFN_GROUPS):
                    npc = pc1 - pc0
                    S.dma('pool', [], ['wdn'], lambda q, pc0=pc0, npc=npc: q.dma_start(
                        out=wdn[:, 0:npc, :],
                        in_=w_down[pc0 * 128:pc1 * 128, :].rearrange("(c p) n -> p c n", p=128)))
                    S.op('dve', [], ['gbuf'], lambda v: v.memset(gbuf[:, :, 0:2], 0.0))
                    if gi == 0:
                        load_wup(0)
                        load_wup(1)
                    for pc in range(pc0, pc1):
                        wu = wup[pc % 3]
                        if pc + 2 < NPC:
                            load_wup(pc + 2)
                        cg = 19 + pc * 3
                        cv_ = 19 + (NPC + pc) * 3
                        bg = 19 + 132 + pc
                        bv = 19 + 132 + NPC + pc
                        for (c0, W) in FW:
                            b = it % 2
                            it += 1
                            pg, pv = psb[2 * b], psb[2 * b + 1]
                            kg, kv = ('ps', 2 * b), ('ps', 2 * b + 1)
                            xkeys = [('xT', jj) for jj in range(c0 // 128, (c0 + W + 127) // 128)]
                            for (ps_, pk, wc0) in ((pg, kg, 0), (pv, kv, 128)):
                                wuk = ('wup', pc % 3, wc0 // 128)
                                for k in range(8):
                                    S.op('pe', [wuk] + xkeys, [pk], lambda t, k=k, ps_=ps_, wc0=wc0: t.matmul(
                                        ps_[:, 0:W], lhsT=wu[:, k, wc0:wc0 + 128], rhs=xT[:, k, c0:c0 + W],
                                        start=(k == 0), stop=(k == 7)))
                            n = W - 2
                            for (ps_, pk, u, uk, tp, bb) in ((pg, kg, ug[b], ('ug', b), cg, bg),
                                                             (pv, kv, uv[b], ('uv', b), cv_, bv)):
                                S.op('act', [pk, 'pp'], [uk], lambda a, ps_=ps_, u=u, tp=tp, bb=bb: a.activation(
                                    out=u[:, 0:n], in_=ps_[:, 2:W], func=AF.Identity,
                                    bias=pp[:, bb:bb + 1], scale=pp[:, tp + 2:tp + 3]))
                                S.op('dve', [pk, uk, 'pp'], [uk], lambda v, ps_=ps_, u=u, tp=tp: v.scalar_tensor_tensor(
                                    out=u[:, 0:n], in0=ps_[:, 1:W - 1], scalar=pp[:, tp + 1:tp + 2],
                                    in1=u[:, 0:n], op0=ALU.mult, op1=ALU.add))
                                S.op('dve', [pk, uk, 'pp'], [uk], lambda v, ps_=ps_, u=u, tp=tp: v.scalar_tensor_tensor(
                                    out=u[:, 0:n], in0=ps_[:, 0:W - 2], scalar=pp[:, tp:tp + 1],
                                    in1=u[:, 0:n], op0=ALU.mult, op1=ALU.add))
                            S.op('act', [('ug', b)], [('sgt', b)], lambda a, b=b: a.activation(
                                out=sgt[b][:, 0:n], in_=ug[b][:, 0:n], func=AF.Silu))
                            S.op('dve', [('sgt', b), ('uv', b)], ['gbuf'], lambda g, b=b, pc=pc: g.tensor_tensor(
                                out=gbuf[:, pc - pc0, c0 + 2:c0 + W], in0=sgt[b][:, 0:n], in1=uv[b][:, 0:n],
                                op=ALU.mult))
                    for j in range(fj0, NTILE):
                        for fb in range(2):
                            pi = 4 + (2 * j + fb) % 3
                            for c in range(npc):
                                S.op('pe', ['gbuf', 'wdn'], [('ps', pi)], lambda t, c=c, pi=pi: t.matmul(
                                    psb[pi][:, :], lhsT=gbuf[:, c, j * 128:(j + 1) * 128],
                                    rhs=wdn[:, c, fb * 512:(fb + 1) * 512], start=(c == 0), stop=(c == npc - 1)))
                            xs = x_res[:, j, fb * 512:(fb + 1) * 512]
                            if gi == 0:
                                S.op('dve', [('xres', j), ('ps', pi)], [('xres', j)], lambda v, pi=pi, xs=xs: v.scalar_tensor_tensor(
                                    out=xs, in0=xs, scalar=ALPHA, in1=psb[pi][:, :], op0=ALU.mult, op1=ALU.add))
                            else:
                                S.op('dve', [('xres', j), ('ps', pi)], [('xres', j)], lambda v, pi=pi, xs=xs: v.tensor_tensor(
                                    out=xs, in0=xs, in1=psb[pi][:, :], op=ALU.add))
                        if gi == len(FFN_GROUPS) - 1:
                            ln_stats2(j)
                ln_apply2(fj0)
                S.barrier()
            S.barrier()

        for layer in range(2):
            r = emit_layer(layer)
            if r is not None:
                return r
        for j in range(1, NTILE):
            S.dma('sp', [('xres', j)], [], lambda q, j=j: q.dma_start(
                out=yw[(j - 1) * 128:j * 128, :], in_=x_res[:, j, :]))
        S.barrier()
    return nc


_PROGS = {}


def _prog(mode):
    if mode not in _PROGS:
        _PROGS[mode] = build_program(mode)
    return _PROGS[mode]


def _consts():
    c = np.zeros((128, 640), np.float32)
    c[:, 0:128] = np.eye(128, dtype=np.float32)
    s = np.arange(128)[:, None]
    t = np.arange(128)[None, :]
    m = ((t >= s) & ((t // 64) == (s // 64))).astype(np.float32)
    for j in range(4):
        c[:, 128 + j * 128:256 + j * 128] = m
    return c


def _pp(l, w_conv, pool_scale, hg_lower_bounds, hg_norm_g, w_ffn_conv, b_ffn_conv):
    p = np.zeros((128, NPP), np.float32)
    p[:, 0:6] = w_conv[l].reshape(2, 128, 3).transpose(1, 0, 2).reshape(128, 6)
    p[:, 6:8] = pool_scale[l].reshape(2, 128).T
    rw = np.zeros((128, 2), np.float32)
    rw[0:64, 0], rw[64:128, 0], rw[0:64, 1], rw[64:128, 1] = 1 / 2, 1 / 4, 1 / 8, 1 / 16
    p[:, 8:10] = rw
    p[:, 10:14] = hg_lower_bounds[0].reshape(4, 128).T
    p[:, 14:18] = hg_lower_bounds[1].reshape(4, 128).T
    p[:, 18] = hg_norm_g[l]
    p[:, 19:19 + 132] = w_ffn_conv[l].reshape(44, 128, 3).transpose(1, 0, 2).reshape(128, 132)
    p[:, 19 + 132:] = b_ffn_conv[l].reshape(44, 128).T
    return p


def _rc(first):
    r = np.zeros((128, 32), np.float32)
    wins = {(0, 0): 2, (0, 1): 4, (1, 0): 8, (1, 1): 16}
    for (j, hf), win in wins.items():
        for i in range(16):
            cnt = min(i + 1, win) if first else win
            r[hf * 64:(hf + 1) * 64, j * 16 + i] = 1.0 / cnt
    return r


def kernel(x, meta_tokens, hg_lower_bounds, w_in, w_conv, w_pool, pool_scale, hg_norm_g,
           w_o, ln1_g, ln1_b, w_up, w_ffn_conv, b_ffn_conv, w_down, ln2_g, ln2_b):
    f = lambda a: np.ascontiguousarray(np.asarray(a, dtype=np.float32))
    x, meta_tokens, hg_lower_bounds, w_in, w_conv, w_pool, pool_scale, hg_norm_g, w_o, ln1_g, \
        ln1_b, w_up, w_ffn_conv, b_ffn_conv, w_down, ln2_g, ln2_b = map(f, (
            x, meta_tokens, hg_lower_bounds, w_in, w_conv, w_pool, pool_scale, hg_norm_g, w_o,
            ln1_g, ln1_b, w_up, w_ffn_conv, b_ffn_conv, w_down, ln2_g, ln2_b))
    ncores = 8
    cores = list(range(ncores))
    consts = _consts()
    shared = {"consts": consts}
    for l in range(2):
        shared["w_in%d" % l] = w_in[l]
        shared["pp%d" % l] = _pp(l, w_conv, pool_scale, hg_lower_bounds, hg_norm_g, w_ffn_conv, b_ffn_conv)
        shared["w_o%d" % l] = w_o[l]
        shared["w_up%d" % l] = w_up[l]
        shared["w_down%d" % l] = w_down[l]
        shared["w_pool%d" % l] = w_pool[l]
        shared["lnp%d" % l] = np.ascontiguousarray(np.stack([
            np.broadcast_to(v[l][None, :], (128, D)) for v in (ln1_g, ln1_b, ln2_g, ln2_b)]))
    maps = []
    for c in cores:
        b, r = divmod(c, 4)
        own = x[b, r * OWN:(r + 1) * OWN]
        if r == 0:
            halo = np.zeros((HALO, D), np.float32)
            halo[HALO - 16:] = meta_tokens
        else:
            halo = x[b, r * OWN - HALO:r * OWN]
        sel = np.zeros((128, 4), np.float32)
        sel[:, :r] = 1.0
        m0 = np.ones((128, 1), np.float32)
        if r == 0:
            m0[:HALO - 16] = 0.0
        m = dict(shared)
        m.update({"xw": np.ascontiguousarray(np.concatenate([halo, own], axis=0)),
                  "rc": _rc(r == 0), "mask0": m0, "sel": sel})
        maps.append(m)
    res = run_bass_kernel_spmd(_prog('fused'), maps, core_ids=cores)
    out = np.zeros((2, 4 * OWN, D), np.float32)
    for c in cores:
        b, r = divmod(c, 4)
        out[b, r * OWN:(r + 1) * OWN] = res.results[c]["yw"]
    return out
```

```python
import contextlib
import numpy as np
import concourse.bass as bass
import concourse.mybir as mybir
from concourse.bass_utils import run_bass_kernel_spmd

F32 = mybir.dt.float32
BF16 = mybir.dt.bfloat16
AF = mybir.ActivationFunctionType
ALU = mybir.AluOpType
AX = mybir.AxisListType

D = 1024
NT = 2176
NTILE = 17
OWN = 2048
HALO = 128
DFF = 2816
DIN = 3072
NPC = 22
ALPHA = float(4.0 ** 0.25)
LN_EPS = 1e-5
RMS_EPS = 1e-6
C_CB, C_CC, C_CV, C_Q, C_F, C_I, C_G, C_PV = 0, 256, 512, 768, 1280, 1792, 2304, 2816
FFN_GROUPS = [(0, 6), (6, 11), (11, 17), (17, 22)]
NPP = 19 + 132 + 44
WINS = [(0, 512), (512, 512), (1024, 512), (1536, 512), (2048, 128)]
NDMA = 24


class Sched:
    def __init__(self, nc, es):
        self.nc = nc
        self.eng = {'pe': nc.tensor, 'act': nc.scalar, 'dve': nc.vector,
                    'pool': nc.gpsimd, 'sp': nc.sync}
        self.semh = {}
        for e in ['pe', 'act', 'dve', 'pool']:
            self.semh[e] = es.enter_context(nc.semaphore("sem_" + e))
        for j in range(NDMA):
            self.semh[('dma', j)] = es.enter_context(nc.semaphore("semd%d" % j))
        self.semh['cc'] = es.enter_context(nc.semaphore("sem_cc"))
        self.cnt = {e: 0 for e in ['pe', 'act', 'dve', 'pool', 'cc']}
        self.known = {e: {} for e in self.eng}
        self.lastw = {}
        self.readers = {}
        self.dma_use = [0] * NDMA
        self.dma_nextq = [0, 0]
        self.latest = {}
        self.cap = None

    def _wait(self, e, tok):
        if tok is None:
            return
        sk, v = tok
        if e == 'pe' and sk == 'pe':
            return
        if self.known[e].get(sk, 0) >= v:
            return
        self.eng[e].wait_ge(self.semh[sk], v)
        self.known[e][sk] = v

    def deps(self, e, reads, writes):
        for k in reads:
            self._wait(e, self.lastw.get(k))
        for k in writes:
            self._wait(e, self.lastw.get(k))
            for sk, v in self.readers.get(k, {}).items():
                self._wait(e, (sk, v))

    def commit(self, tok, reads, writes):
        sk, v = tok
        self.latest[sk] = max(self.latest.get(sk, 0), v)
        for k in reads:
            r = self.readers.setdefault(k, {})
            r[sk] = max(r.get(sk, 0), v)
        for k in writes:
            self.lastw[k] = tok
            self.readers[k] = {}

    def capture_ops(self, body):
        self.cap = []
        body()
        out, self.cap = self.cap, None
        return out

    def op(self, e, reads, writes, fn):
        if self.cap is not None:
            self.cap.append(lambda: self._op(e, reads, writes, fn))
            return
        self._op(e, reads, writes, fn)

    def _op(self, e, reads, writes, fn):
        self.deps(e, reads, writes)
        ins = fn(self.eng[e])
        self.cnt[e] += 1
        ins.then_inc(self.semh[e], 1)
        self.commit((e, self.cnt[e]), reads, writes)

    def dma(self, q, reads, writes, fn):
        if self.cap is not None:
            self.cap.append(lambda: self._dma(q, reads, writes, fn))
            return
        self._dma(q, reads, writes, fn)

    def _dma(self, q, reads, writes, fn):
        self.deps(q, reads, writes)
        half = NDMA // 2
        qi = 0 if q == 'sp' else 1
        j = qi * half + self.dma_nextq[qi]
        self.dma_nextq[qi] = (self.dma_nextq[qi] + 1) % half
        r = self.dma_use[j]
        if r > 0:
            self._wait(q, (('dma', j), 16 * r))
        ins = fn(self.eng[q])
        ins.then_inc(self.semh[('dma', j)], 16)
        self.dma_use[j] += 1
        self.commit((('dma', j), 16 * (r + 1)), reads, writes)

    def cc(self, q, reads, writes, fn):
        self.deps(q, reads, writes)
        ins = fn(self.eng[q])
        self.cnt['cc'] += 1
        ins.then_inc(self.semh['cc'])
        self.commit(('cc', self.cnt['cc']), reads, writes)

    def barrier(self, engines=('pe', 'act', 'dve', 'pool', 'sp')):
        for e in engines:
            for sk, v in list(self.latest.items()):
                self._wait(e, (sk, v))


def build_program(mode='fused', stop=None):
    nc = bass.Bass("TRN2", target_bir_lowering=False)
    es = contextlib.ExitStack()

    def din(name, shape, dt=F32):
        return nc.dram_tensor(name, list(shape), dt, kind="ExternalInput").ap()

    def dout(name, shape, dt=F32):
        return nc.dram_tensor(name, list(shape), dt, kind="ExternalOutput").ap()

    xw = din("xw", [NT, D])
    consts_d = din("consts", [128, 640])
    rc_d = din("rc", [128, 32])
    mask0_d = din("mask0", [128, 1])
    sel_d = din("sel", [128, 4])
    WD = []
    for l in range(2):
        WD.append(dict(
            w_in=din("w_in%d" % l, [D, DIN]), pp=din("pp%d" % l, [128, NPP]),
            w_o=din("w_o%d" % l, [D, D]), w_up=din("w_up%d" % l, [D, 2 * DFF]),
            w_down=din("w_down%d" % l, [DFF, D]), w_pool=din("w_pool%d" % l, [4, 64, 64]),
            lnp=din("lnp%d" % l, [4, 128, D]),
            xin=nc.dram_tensor("xchg_in%d" % l, [128, 516], F32),
            xout=nc.dram_tensor("xchg_out%d" % l, [512, 516], F32)))
    yw = dout("yw", [OWN, D])

    with es:
        S = Sched(nc, es)

        sfx = [""]

        uid = [0]

        def sb(name, shape, dt=F32, stack=es):
            uid[0] += 1
            return stack.enter_context(nc.sbuf_tensor("%s%s_%d" % (name, sfx[0], uid[0]), list(shape), dt))

        x_res = sb("x_res", [128, NTILE, D])
        xT = sb("xT", [128, 8, NT], BF16)
        pp = sb("pp_sb", [128, NPP])
        cst = sb("cst", [128, 640])
        ident = sb("ident", [128, 128], BF16)
        onesb = sb("onesb", [128, 128], BF16)
        epsb = sb("epsb", [128, 3])
        mscan = sb("mscan", [128, 512])
        lbt = sb("lbt", [128, 8])
        noml = sb("noml", [128, 4])
        lb2 = sb("lb2", [128, 12])
        gn2 = sb("gn2", [128, 1])
        zbf = sb("zbf", [128, 128], BF16)
        psb = [es.enter_context(nc.psum_tensor("ps%d" % i, [128, 512], F32)) for i in range(7)]
        pst = es.enter_context(nc.psum_tensor("pst", [128, 1024], BF16))
        mask4 = cst[:, 128:640]
        mask0 = sb("mask0_sb", [128, 1])
        if True:
            S.dma('sp', [], ['mask0'], lambda q: q.dma_start(out=mask0[:, :], in_=mask0_d[:, :]))

        S.dma('sp', [], ['cst'], lambda q: q.dma_start(out=cst[:, :], in_=consts_d[:, :]))
        S.op('dve', ['cst'], ['ident'], lambda v: v.tensor_copy(out=ident[:, :], in_=cst[:, 0:128]))
        S.op('dve', [], ['onesb'], lambda v: v.memset(onesb[:, :], 1.0))
        S.op('dve', [], ['epsb'], lambda v: v.memset(epsb[:, 0:1], float(128.0 * RMS_EPS)))
        S.op('dve', ['epsb'], ['epsb'], lambda v: v.memset(epsb[:, 1:2], float(LN_EPS)))
        S.op('dve', ['epsb'], ['epsb'], lambda v: v.memset(epsb[:, 2:3], 1.0))
        S.op('dve', [], ['zbf'], lambda v: v.memset(zbf[:, :], 0.0))
        S.op('dve', [], ['mscan'], lambda v: v.memset(mscan[:, :], 1.0))
        S.op('dve', ['mscan'], ['mscan'], lambda v: v.memset(
            mscan[:, :].rearrange("p (c s) -> p c s", s=64)[:, :, 0:1], 0.0))
        def tile_to_xT(j, src_bf, src_key):
            for k in range(8):
                S.op('pe', [src_key, 'ident'], [('pst',)], lambda t, k=k: t.transpose(
                    out=pst[:, k * 128:(k + 1) * 128], in_=src_bf[:, k * 128:(k + 1) * 128],
                    identity=ident[:, :]))
            S.op('act', [('pst',)], [('xT', j)], lambda a: a.activation(
                out=xT[:, :, j * 128:(j + 1) * 128],
                in_=pst[:, :].rearrange("p (k t) -> p k t", t=128), func=AF.Copy))

        def wblock_load(dst, dst_key, src, col0, ncols, dcol0):
            S.dma('pool', [], [dst_key], lambda q: q.dma_start(
                out=dst[:, :, dcol0:dcol0 + ncols],
                in_=src.rearrange("(k p) n -> p k n", p=128)[:, :, col0:col0 + ncols]))

        def inproj_fm(ps, ps_key, wt, wt_key, wcol, t0, W):
            xkeys = [('xT', j) for j in range(t0 // 128, (t0 + W + 127) // 128)]
            for k in range(8):
                S.op('pe', [wt_key] + xkeys, [ps_key], lambda t, k=k: t.matmul(
                    ps[:, 0:W], lhsT=wt[:, k, wcol:wcol + 128], rhs=xT[:, k, t0:t0 + W],
                    start=(k == 0), stop=(k == 7)))

        def finish():
            for j in range(1, NTILE):
                S.dma('sp', [('xres', j)], [], lambda q, j=j: q.dma_start(
                    out=yw[(j - 1) * 128:j * 128, :], in_=x_res[:, j, :]))
            S.barrier()
            return nc

        if True:
            xb = [sb("xb%d" % i, [128, D], BF16) for i in range(2)]
            for j in range(NTILE):
                S.dma('sp', [], [('xres', j)], lambda q, j=j: q.dma_start(
                    out=x_res[:, j, :], in_=xw[j * 128:(j + 1) * 128, :]))
            for j in range(NTILE):
                b = xb[j % 2]
                S.op('dve', [('xres', j)], [('xb', j % 2)], lambda v, j=j, b=b: v.tensor_copy(
                    out=b[:, :], in_=x_res[:, j, :]))
                tile_to_xT(j, b, ('xb', j % 2))

        def emit_layer(layer):
            sfx[0] = '_L%d' % layer
            Wl = WD[layer]
            w_in, w_o, w_up, w_down, w_pool, lnp = (Wl[k] for k in ('w_in', 'w_o', 'w_up', 'w_down', 'w_pool', 'lnp'))
            xin, xout = Wl['xin'].ap(), Wl['xout'].ap()
            S.dma('sp', [], ['pp'], lambda q: q.dma_start(out=pp[:, :], in_=Wl['pp'][:, :]))
            if layer == 0:
                S.op('dve', [], ['lbt'], lambda v: v.memset(lbt[:, 0:4], 0.0))
                S.op('dve', ['lbt'], ['lbt'], lambda v: v.memset(lbt[:, 4:8], 1.0))
            else:
                S.op('dve', ['pp'], ['lbt'], lambda v: v.tensor_tensor(
                    out=lbt[:, 4:8], in0=pp[:, 14:18], in1=pp[:, 10:14], op=ALU.subtract))
                S.op('act', ['lbt'], ['lbt'], lambda a: a.activation(
                    out=lbt[:, 0:4], in_=lbt[:, 4:8], func=AF.Sigmoid))
                S.op('dve', ['lbt'], ['lbt'], lambda v: v.tensor_scalar(
                    out=lbt[:, 4:8], in0=lbt[:, 0:4], scalar1=-1.0, scalar2=1.0,
                    op0=ALU.mult, op1=ALU.add))
            S.op('dve', ['lbt'], ['noml'], lambda v: v.tensor_scalar(
                out=noml[:, :], in0=lbt[:, 4:8], scalar1=-1.0, scalar2=None, op0=ALU.mult))
            S.op('dve', ['lbt'], ['lbt'], lambda v: v.tensor_scalar(
                out=lb2[:, 0:4], in0=lbt[:, 4:8], scalar1=0.5, scalar2=None, op0=ALU.mult))
            S.op('dve', ['lbt'], ['lbt'], lambda v: v.tensor_tensor(
                out=lb2[:, 4:8], in0=lb2[:, 0:4], in1=lbt[:, 0:4], op=ALU.add))
            S.op('dve', ['lbt'], ['lbt'], lambda v: v.tensor_scalar(
                out=lb2[:, 8:12], in0=lbt[:, 4:8], scalar1=-0.5, scalar2=None, op0=ALU.mult))
            S.op('dve', ['pp'], ['gn2'], lambda v: v.tensor_scalar(
                out=gn2[:, :], in0=pp[:, 18:19], scalar1=float(128.0 ** 0.5), scalar2=None,
                op0=ALU.mult))

            def hgrn2(ph, full, mixT, s_in, wb=None, preloaded=()):
                if wb is None:
                    wb = [sb("hwb%d" % i, [128, 8, 512], BF16, ph) for i in range(2)]
                names = ['sig', 'f', 'G', 'E2'] + (['q', 'E1', 'o', 'rs', 'o2'] if full else [])
                T = {n: sb("hT_" + n, [128, 512], F32, ph) for n in names}
                Bi2 = [sb("hBi%d" % i, [128, 512], BF16, ph) for i in range(2)]
                Bkh2 = [sb("hBkh%d" % i, [128, 512], BF16, ph) for i in range(2)]
                dec2 = [sb("hdec%d" % i, [128, 8], F32, ph) for i in range(2)]
                Bq2 = Bk2 = [None, None]
                tok4 = sb("htok4", [128, 4, 256], BF16, ph)
                Sf = [sb("hSf%d" % i, [128, 128], F32, ph) for i in range(2)]
                NSB = 12
                Sb = sb("hSb", [128, NSB, 128], BF16, ph)
                gsum = sb("hgsum", [128, 4], F32, ph)
                gtmp = sb("hgtmp", [128, 1], F32, ph)
                if full:
                    Tsg = [sb("hTsg%d" % i, [128, 512], F32, ph) for i in range(2)]
                    Bq2 = [sb("hBq%d" % i, [128, 512], BF16, ph) for i in range(2)]
                    Bk2 = [sb("hBk%d" % i, [128, 512], BF16, ph) for i in range(2)]
                    Am4 = sb("hAm4", [128, 512], BF16, ph)
                    o2h = sb("ho2h", [128, 512], BF16, ph)
                    o2l = sb("ho2l", [128, 512], BF16, ph)
                n0 = layer
                S.op('dve', [], ['gsum'], lambda v: v.memset(gsum[:, :], 0.0))

                def load_head(h):
                    w = wb[h % 2]
                    blocks = [(C_F, 128, 1), (C_I, 256, 2)]
                    if full:
                        blocks = [(C_Q, 0, 0), (C_G, 384, 3)] + blocks
                    for (cbase, dcol, bi) in blocks:
                        wblock_load(w, ('hwb', h % 2, bi), w_in, cbase + h * 128, 128, dcol)

                items = []
                for h in range(4):
                    for wi, (t0, W) in enumerate(WINS):
                        chunks = [8 * wi + c for c in range(W // 64)]
                        need_state = [n for n in chunks if n0 <= n < n0 + 32]
                        if full or need_state:
                            items.append((h, wi, t0, W, need_state))

                def inproj(it):
                    h, wi, t0, W, need_state = it
                    w = wb[h % 2]
                    inproj_fm(psb[1], ('ps', 1), w, ('hwb', h % 2, 1), 128, t0, W)
                    inproj_fm(psb[2], ('ps', 2), w, ('hwb', h % 2, 2), 256, t0, W)
                    if full:
                        inproj_fm(psb[0], ('ps', 0), w, ('hwb', h % 2, 0), 0, t0, W)
                        inproj_fm(psb[3], ('ps', 3), w, ('hwb', h % 2, 3), 384, t0, W)

                def pre_stage(it, par, part='ab'):
                    h, wi, t0, W, need_state = it
                    nch = W // 64
                    Bq, Bk, Bi, Bkh, dec = Bq2[par], Bk2[par], Bi2[par], Bkh2[par], dec2[par]
                    if 'a' in part:
                        if full:
                            S.op('act', [('ps', 0)], ['hq'], lambda a: a.activation(
                                out=T['q'][:, 0:W], in_=psb[0][:, 0:W], func=AF.Copy,
                                scale=float(128.0 ** -0.5)))
                        if full:
                            S.op('act', [('ps', 1)], ['hsig'], lambda a: a.activation(
                                out=T['sig'][:, 0:W], in_=psb[1][:, 0:W], func=AF.Tanh, scale=0.5))
                        else:
                            S.op('act', [('ps', 1)], ['hsig'], lambda a: a.activation(
                                out=T['sig'][:, 0:W], in_=psb[1][:, 0:W], func=AF.Exp, scale=-1.0))
                        S.op('act', [('ps', 2)], [('hBi', par)], lambda a: a.activation(
                            out=Bi[:, 0:W], in_=psb[2][:, 0:W], func=AF.Copy))
                        if full:
                            S.op('act', [('ps', 3)], [('hsg', par)], lambda a: a.activation(
                                out=Tsg[par][:, 0:W], in_=psb[3][:, 0:W], func=AF.Silu))
                    if 'b' not in part:
                        return
                    if full:
                        S.op('dve', ['hsig', 'lbt'], ['hf'], lambda v: v.tensor_scalar(
                            out=T['f'][:, 0:W], in0=T['sig'][:, 0:W], scalar1=lb2[:, h:h + 1],
                            scalar2=lb2[:, 4 + h:5 + h], op0=ALU.mult, op1=ALU.add))
                        S.op('dve', ['hf'], ['hf'], lambda v: v.tensor_scalar(
                            out=T['f'][:, 0:W], in0=T['f'][:, 0:W], scalar1=1e-30, scalar2=None,
                            op0=ALU.max))
                        S.op('dve', ['hsig', 'lbt'], ['hsig'], lambda v: v.tensor_scalar(
                            out=T['sig'][:, 0:W], in0=T['sig'][:, 0:W], scalar1=-1.0,
                            scalar2=lb2[:, 8 + h:9 + h], op0=ALU.add, op1=ALU.mult))
                        S.op('act', ['hf'], ['hf'], lambda a: a.activation(
                            out=T['f'][:, 0:W], in_=T['f'][:, 0:W], func=AF.Ln))
                    else:
                        S.op('act', ['hsig', 'lbt', 'epsb'], ['hf'], lambda a: a.activation(
                            out=T['f'][:, 0:W], in_=T['sig'][:, 0:W], func=AF.Ln,
                            scale=lbt[:, h:h + 1], bias=epsb[:, 2:3]))
                        S.op('act', ['hsig', 'epsb'], ['hG'], lambda a: a.activation(
                            out=T['G'][:, 0:W], in_=T['sig'][:, 0:W], func=AF.Ln, bias=epsb[:, 2:3]))
                        S.op('dve', ['hf', 'hG'], ['hf'], lambda v: v.tensor_tensor(
                            out=T['f'][:, 0:W], in0=T['f'][:, 0:W], in1=T['G'][:, 0:W], op=ALU.subtract))
                        S.op('dve', ['hf'], ['hf'], lambda v: v.tensor_scalar(
                            out=T['f'][:, 0:W], in0=T['f'][:, 0:W], scalar1=-69.07755279, scalar2=None,
                            op0=ALU.max))
                        S.op('act', ['hG'], ['hE2'], lambda a: a.activation(
                            out=T['E2'][:, 0:W], in_=T['G'][:, 0:W], func=AF.Exp, scale=-1.0))
                        S.op('dve', ['hsig', 'lbt', 'hE2'], ['hsig'], lambda v: v.scalar_tensor_tensor(
                            out=T['sig'][:, 0:W], in0=T['sig'][:, 0:W], scalar=lbt[:, 4 + h:5 + h],
                            in1=T['E2'][:, 0:W], op0=ALU.mult, op1=ALU.mult))
                    S.op('dve', ['hf', 'mscan'], ['hG'], lambda v: v.tensor_tensor_scan(
                        out=T['G'][:, 0:W], data0=mscan[:, 0:W], data1=T['f'][:, 0:W],
                        initial=0.0, op0=ALU.mult, op1=ALU.add))
                    Glast = T['G'][:, 0:W].rearrange("p (c s) -> p c s", s=64)[:, :, 63:64]
                    S.op('act', ['hG'], ['hE2'], lambda a: a.activation(
                        out=T['E2'][:, 0:W], in_=T['G'][:, 0:W], func=AF.Exp, scale=-1.0))
                    S.op('act', ['hG'], [('hdec', par)], lambda a: a.activation(
                        out=dec[:, 0:nch].rearrange("p (c o) -> p c o", o=1), in_=Glast,
                        func=AF.Exp))
                    if full:
                        S.op('act', ['hG'], ['hE1'], lambda a: a.activation(
                            out=T['E1'][:, 0:W], in_=T['G'][:, 0:W], func=AF.Exp))
                        S.op('dve', ['hq', 'hE1'], [('hBq', par)], lambda v: v.tensor_tensor(
                            out=Bq[:, 0:W], in0=T['q'][:, 0:W], in1=T['E1'][:, 0:W], op=ALU.mult))
                    S.op('dve', ['hsig', 'hE2'], ['hE2'], lambda v: v.tensor_tensor(
                        out=T['E2'][:, 0:W], in0=T['sig'][:, 0:W], in1=T['E2'][:, 0:W], op=ALU.mult))
                    if full:
                        S.op('dve', ['hE2'], [('hBk', par)], lambda v: v.tensor_copy(
                            out=Bk[:, 0:W], in_=T['E2'][:, 0:W]))
                    S.op('dve', ['hE2', ('hdec', par)], [('hBkh', par)], lambda v: v.tensor_tensor(
                        out=Bkh[:, 0:W].rearrange("p (c s) -> p c s", s=64),
                        in0=T['E2'][:, 0:W].rearrange("p (c s) -> p c s", s=64),
                        in1=dec[:, 0:nch].unsqueeze(2).to_broadcast([128, nch, 64]), op=ALU.mult))
                    if not full and need_state:
                        c0 = need_state[0] - 8 * wi
                        c1 = need_state[-1] - 8 * wi + 1
                        S.op('dve', ['hG'], ['gtmp'], lambda v: v.tensor_reduce(
                            out=gtmp[:, :], in_=Glast[:, c0:c1, :], axis=AX.XY, op=ALU.add))
                        S.op('dve', ['gtmp', 'gsum'], ['gsum'], lambda v: v.tensor_tensor(
                            out=gsum[:, h:h + 1], in0=gsum[:, h:h + 1], in1=gtmp[:, :], op=ALU.add))

                def o_evac(it, par):
                    h, wi, t0, W, need_state = it
                    S.op('act', [('ps', 5)], ['ho'], lambda a: a.activation(
                        out=T['o'][:, 0:W], in_=psb[5][:, 0:W], func=AF.Copy))
                    S.op('act', [('ps', 5)], ['ho2'], lambda a: a.activation(
                        out=T['o2'][:, 0:W], in_=psb[5][:, 0:W], func=AF.Square))
                    S.op('dve', ['ho2'], ['ho2h'], lambda v: v.tensor_copy(out=o2h[:, 0:W], in_=T['o2'][:, 0:W]))
                    S.op('dve', ['ho2', 'ho2h'], ['ho2l'], lambda v: v.tensor_tensor(
                        out=o2l[:, 0:W], in0=T['o2'][:, 0:W], in1=o2h[:, 0:W], op=ALU.subtract))
                    S.op('pe', ['ho2h', 'onesb'], [('ps', 6)], lambda t: t.matmul(
                        psb[6][:, 0:W], lhsT=onesb[:, :], rhs=o2h[:, 0:W], start=True, stop=False))
                    S.op('pe', ['ho2l', 'onesb'], [('ps', 6)], lambda t: t.matmul(
                        psb[6][:, 0:W], lhsT=onesb[:, :], rhs=o2l[:, 0:W], start=False, stop=True))
                    S.op('act', [('ps', 6), 'epsb'], ['hrs'], lambda a: a.activation(
                        out=T['rs'][:, 0:W], in_=psb[6][:, 0:W], func=AF.Ln, bias=epsb[:, 0:1]))
                    S.op('act', ['hrs'], ['hrs'], lambda a: a.activation(
                        out=T['rs'][:, 0:W], in_=T['rs'][:, 0:W], func=AF.Exp, scale=-0.5))
                    S.op('dve', ['ho', 'hrs'], ['ho'], lambda v: v.tensor_tensor(
                        out=T['o'][:, 0:W], in0=T['o'][:, 0:W], in1=T['rs'][:, 0:W], op=ALU.mult))
                    mk = [('mixT', jj) for jj in range(t0 // 128, (t0 + W) // 128)]
                    S.op('dve', ['ho', ('hsg', par), 'gn2'], mk, lambda v: v.scalar_tensor_tensor(
                        out=mixT[:, 2 + h, t0:t0 + W], in0=T['o'][:, 0:W], scalar=gn2[:, 0:1],
                        in1=Tsg[par][:, 0:W], op0=ALU.mult, op1=ALU.mult))

                st = dict(sfi=0, sbi=0, valid=False, h=-1)

                def tile_stage(it, par):
                    h, wi, t0, W, need_state = it
                    nch, ntl = W // 64, W // 128
                    Bq, Bk, Bi, Bkh = Bq2[par], Bk2[par], Bi2[par], Bkh2[par]
                    for j in range(ntl):
                        cs = slice(j * 128, (j + 1) * 128)
                        S.op('pe', [('hBi', par), 'ident'], [('pst',)], lambda t, cs=cs, j=j: t.transpose(
                            out=pst[:, j * 256:j * 256 + 128], in_=Bi[:, cs], identity=ident[:, :]))
                        S.op('pe', [('hBkh', par), 'ident'], [('pst',)], lambda t, cs=cs, j=j: t.transpose(
                            out=pst[:, j * 256 + 128:(j + 1) * 256], in_=Bkh[:, cs], identity=ident[:, :]))
                    S.op('act', [('pst',)], ['htok'], lambda a: a.activation(
                        out=tok4[:, 0:ntl, :], in_=pst[:, 0:ntl * 256].rearrange("p (j c) -> p j c", c=256),
                        func=AF.Copy))
                    if full:
                        for j in range(ntl):
                            cs = slice(j * 128, (j + 1) * 128)
                            S.op('pe', [('hBk', par), ('hBq', par)], [('ps', 4)], lambda t, cs=cs: t.matmul(
                                psb[4][:, cs], lhsT=Bk[:, cs], rhs=Bq[:, cs], start=True, stop=True))
                        S.op('dve', [('ps', 4), 'cst'], ['hAm'], lambda v: v.tensor_tensor(
                            out=Am4[:, 0:W], in0=psb[4][:, 0:W], in1=mask4[:, 0:W], op=ALU.mult))
                    upd = []
                    for c in range(nch):
                        n = 8 * wi + c
                        if n < n0:
                            continue
                        if not full and not (n0 <= n < n0 + 32):
                            continue
                        upd.append(c)
                    for c in upd:
                        j, c2 = divmod(c, 2)
                        rows = slice(c2 * 64, (c2 + 1) * 64)
                        bank = 5 + c % 2
                        S.op('pe', ['htok'], [('ps', bank)], lambda t, c=c, j=j, rows=rows, bank=bank: t.matmul(
                            psb[bank][:, (c // 2) * 128:(c // 2 + 1) * 128], lhsT=tok4[rows, j, 128:256],
                            rhs=tok4[rows, j, 0:128], start=True, stop=True))
                    return upd

                def chain_stage(it, par, upd):
                    h, wi, t0, W, need_state = it
                    nch, ntl = W // 64, W // 128
                    Bq, dec = Bq2[par], dec2[par]
                    st_for_chunk = []
                    for c in range(nch):
                        n = 8 * wi + c
                        if n == n0:
                            sfi, sbi = st['sfi'], st['sbi']
                            if s_in is not None:
                                S.op('dve', [('sin', h)], [('hSf', sfi)], lambda v, sfi=sfi: v.tensor_copy(
                                    out=Sf[sfi][:, :], in_=s_in[:, h, :]))
                            else:
                                S.op('dve', [], [('hSf', sfi)], lambda v, sfi=sfi: v.memset(Sf[sfi][:, :], 0.0))
                            if full:
                                S.op('dve', [('hSf', sfi)], [('hSb', sbi)], lambda v, sfi=sfi, sbi=sbi: v.tensor_copy(
                                    out=Sb[:, sbi, :], in_=Sf[sfi][:, :]))
                            st['valid'] = True
                        if not st['valid']:
                            st_for_chunk.append(None)
                            continue
                        st_for_chunk.append(st['sbi'])
                        if c not in upd:
                            continue
                        sfi, sbi = st['sfi'], st['sbi']
                        nsf, nsb = 1 - sfi, (sbi + 1) % NSB
                        bank = 5 + c % 2
                        S.op('dve', [('hSf', sfi), ('hdec', par), ('ps', bank)], [('hSf', nsf)],
                             lambda v, c=c, sfi=sfi, nsf=nsf, bank=bank: v.scalar_tensor_tensor(
                                 out=Sf[nsf][:, :], in0=Sf[sfi][:, :], scalar=dec[:, c:c + 1],
                                 in1=psb[bank][:, (c // 2) * 128:(c // 2 + 1) * 128],
                                 op0=ALU.mult, op1=ALU.add))
                        if full:
                            S.op('dve', [('hSf', nsf)], [('hSb', nsb)], lambda v, nsf=nsf, nsb=nsb: v.tensor_copy(
                                out=Sb[:, nsb, :], in_=Sf[nsf][:, :]))
                        st['sfi'], st['sbi'] = nsf, nsb
                        if not full and n == n0 + 31:
                            S.dma('sp', [('hSf', nsf)], ['xin'], lambda q, nsf=nsf: q.dma_start(
                                out=xin[:, h * 128:(h + 1) * 128], in_=Sf[nsf][:, :]))
                    if not full:
                        return
                    for j in range(ntl):
                        cs = slice(j * 128, (j + 1) * 128)
                        S.op('pe', ['htok', 'hAm'], [('ps', 5)], lambda t, j=j, cs=cs: t.matmul(
                            psb[5][:, cs], lhsT=tok4[:, j, 0:128], rhs=Am4[:, cs], start=True, stop=False))
                        for c2 in range(2):
                            sidx = st_for_chunk[2 * j + c2]
                            lhs = zbf[:, :] if sidx is None else Sb[:, sidx, :]
                            lk = 'zbf' if sidx is None else ('hSb', sidx)
                            S.op('pe', [lk, ('hBq', par)], [('ps', 5)], lambda t, j=j, c2=c2, lhs=lhs: t.matmul(
                                psb[5][:, j * 128 + c2 * 64:j * 128 + (c2 + 1) * 64], lhsT=lhs,
                                rhs=Bq[:, j * 128 + c2 * 64:j * 128 + (c2 + 1) * 64],
                                start=False, stop=(c2 == 1)))

                loaded = set(preloaded)
                if 0 not in loaded:
                    load_head(0)
                    loaded.add(0)
                inproj(items[0])
                pre_stage(items[0], 0, 'a')
                if len(items) > 1:
                    if items[1][0] not in loaded:
                        load_head(items[1][0])
                        loaded.add(items[1][0])
                    inproj(items[1])
                pre_stage(items[0], 0, 'b')
                for ii, it in enumerate(items):
                    h = it[0]
                    par = ii % 2
                    if h != st['h']:
                        st.update(h=h, valid=False)
                        if h + 1 < 4 and (h + 1) not in loaded:
                            load_head(h + 1)
                            loaded.add(h + 1)
                    nxt = items[ii + 1] if ii + 1 < len(items) else None
                    nn = items[ii + 2] if ii + 2 < len(items) else None
                    upd = tile_stage(it, par)
                    if nxt is not None:
                        pre_stage(nxt, 1 - par, 'a')
                    if nn is not None:
                        if nn[0] not in loaded:
                            load_head(nn[0])
                            loaded.add(nn[0])
                        inproj(nn)
                    A = S.capture_ops(lambda: pre_stage(nxt, 1 - par, 'b')) if nxt is not None else []
                    B = S.capture_ops(lambda: o_evac(it, par)) if full else []
                    na_exp = 3 if full else 2

                    def run(lst, a, b):
                        for t_ in lst[a:b]:
                            t_()
                    run(A, 0, 4)
                    chain_stage(it, par, upd)
                    run(B, 0, 2)
                    run(A, 4, 5)
                    run(B, 2, 6)
                    run(A, 5, 5 + na_exp)
                    run(B, 6, 8)
                    run(A, 5 + na_exp, len(A))
                    run(B, 8, len(B))
                if not full:
                    S.dma('sp', ['gsum'], ['xin'], lambda q: q.dma_start(out=xin[:, 512:516], in_=gsum[:, :]))

            with contextlib.ExitStack() as ph:
                hgrn2(ph, False, None, None)
                S.barrier()
            S.cc('pool', ['xin'], ['xout'], lambda g: g.collective_compute(
                "AllGather", ALU.bypass, replica_groups=[[0, 1, 2, 3], [4, 5, 6, 7]],
                ins=[xin.opt()], outs=[xout.opt()]))

            with contextlib.ExitStack() as mph:
                mixT = sb("mixT", [128, 8, NT], BF16, mph)
                with contextlib.ExitStack() as ph:
                    cwb = [sb("cwb%d" % i, [128, 8, 384], BF16, ph) for i in range(2)]
                    pwb = [sb("pwb%d" % i, [128, 8, 128], BF16, ph) for i in range(2)]
                    for j in range(2):
                        for bi, cbase in enumerate((C_CB, C_CC, C_CV)):
                            wblock_load(cwb[j], ('cwb', j, bi), w_in, cbase + j * 128, 128, bi * 128)
                    for j in range(2):
                        wblock_load(pwb[j], ('pwb', j), w_in, C_PV + j * 128, 128, 0)
                    A = sb("cA", [128, 16 + NT], F32, ph)
                    Bb = sb("cB", [128, 16 + NT], F32, ph)
                    Cc = sb("cC", [128, 16 + NT], F32, ph)
                    cct = [sb("cct%d" % i, [128, 512], F32, ph) for i in range(2)]
                    dbf = sb("cdbf", [128, NT], BF16, ph)
                    wpd = sb("wpd", [128, 2, 128], BF16, ph)
                    rc = sb("rc_sb", [128, 32], F32, ph)
                    S.dma('sp', [], ['rc'], lambda q: q.dma_start(out=rc[:, :], in_=rc_d[:, :]))
                    S.op('dve', [], ['wpd'], lambda v: v.memset(wpd[:, :, :], 0.0))
                    for g in range(4):
                        r0 = (g % 2) * 64
                        S.dma('pool', [], ['wpd'], lambda q, g=g, r0=r0: q.dma_start(
                            out=wpd[r0:r0 + 64, g // 2, r0:r0 + 64], in_=w_pool[g, :, :]))
                    for buf, key in ((A, 'cA'), (Bb, 'cB'), (Cc, 'cC')):
                        S.op('pool', [], [key], lambda g, buf=buf: g.memset(buf[:, 0:16], 0.0))
                    allmix = [('mixT', jj) for jj in range(NTILE)]
                    for j in range(2):
                        w = cwb[j]
                        for wi, (t0, W) in enumerate(WINS):
                            o3 = (wi % 2) * 3
                            inproj_fm(psb[o3 + 0], ('ps', o3 + 0), w, ('cwb', j, 0), 0, t0, W)
                            inproj_fm(psb[o3 + 1], ('ps', o3 + 1), w, ('cwb', j, 1), 128, t0, W)
                            inproj_fm(psb[o3 + 2], ('ps', o3 + 2), w, ('cwb', j, 2), 256, t0, W)
                            ct = cct[wi % 2]
                            S.op('act', [('ps', o3 + 0)], ['cA'], lambda a, o3=o3: a.activation(
                                out=A[:, 16 + t0:16 + t0 + W], in_=psb[o3][:, 0:W], func=AF.Copy))
                            S.op('act', [('ps', o3 + 1)], [('cct', wi % 2)], lambda a, o3=o3, ct=ct: a.activation(
                                out=ct[:, 0:W], in_=psb[o3 + 1][:, 0:W], func=AF.Copy))
                            S.op('dve', [('cct', wi % 2), ('ps', o3 + 2)], ['cB'], lambda v, o3=o3, ct=ct: v.tensor_tensor(
                                out=Bb[:, 16 + t0:16 + t0 + W], in0=ct[:, 0:W], in1=psb[o3 + 2][:, 0:W],
                                op=ALU.mult))
                        wc = lambda kk: pp[:, j * 3 + kk:j * 3 + kk + 1]
                        S.op('dve', ['cB', 'pp'], ['cC'], lambda v: v.tensor_scalar(
                            out=Cc[:, 16:16 + NT], in0=Bb[:, 16:16 + NT], scalar1=wc(2), scalar2=None,
                            op0=ALU.mult))
                        S.op('dve', ['cB', 'cC', 'pp'], ['cC'], lambda v: v.scalar_tensor_tensor(
                            out=Cc[:, 16:16 + NT], in0=Bb[:, 15:15 + NT], scalar=wc(1),
                            in1=Cc[:, 16:16 + NT], op0=ALU.mult, op1=ALU.add))
                        S.op('dve', ['cB', 'cC', 'pp'], ['cC'], lambda v: v.scalar_tensor_tensor(
                            out=Cc[:, 16:16 + NT], in0=Bb[:, 14:14 + NT], scalar=wc(0),
                            in1=Cc[:, 16:16 + NT], op0=ALU.mult, op1=ALU.add))
                        S.op('dve', ['cA', 'cC'], allmix, lambda v: v.tensor_tensor(
                            out=mixT[:, j, :], in0=Cc[:, 16:16 + NT], in1=A[:, 16:16 + NT], op=ALU.mult))
                    for j in range(2):
                        w = pwb[j]
                        for wi, (t0, W) in enumerate(WINS):
                            pi = wi % 2
                            inproj_fm(psb[pi], ('ps', pi), w, ('pwb', j), 0, t0, W)
                            S.op('act', [('ps', pi)], ['cA'], lambda a, pi=pi: a.activation(
                                out=A[:, 16 + t0:16 + t0 + W], in_=psb[pi][:, 0:W], func=AF.Copy))
                        V = A
                        def wsum(dst, dk, src, sk, sh, p0=0):
                            S.op('pool', [sk, dk], [dk], lambda g: g.tensor_tensor(
                                out=dst[p0:128, 16:16 + NT], in0=src[p0:128, 16:16 + NT],
                                in1=src[p0:128, 16 - sh:16 - sh + NT], op=ALU.add))
                        wsum(Bb, 'cB', V, 'cA', 1)
                        if j == 0:
                            wsum(Cc, 'cC', Bb, 'cB', 2, 64)
                            lo, hi = Bb, Cc
                            lok, hik = 'cB', 'cC'
                        else:
                            wsum(Cc, 'cC', Bb, 'cB', 2)
                            wsum(Bb, 'cB', Cc, 'cC', 4)
                            wsum(Cc, 'cC', Bb, 'cB', 8, 64)
                            lo, hi = Bb, Cc
                            lok, hik = 'cB', 'cC'
                        for (p0, p1, src, sk) in ((0, 64, lo, lok), (64, 128, hi, hik)):
                            S.op('dve', [sk, 'cA', 'pp'], ['cdbf'], lambda v, p0=p0, p1=p1, src=src: v.scalar_tensor_tensor(
                                out=dbf[p0:p1, :], in0=src[p0:p1, 16:16 + NT], scalar=pp[p0:p1, 8 + j:9 + j],
                                in1=V[p0:p1, 16:16 + NT], op0=ALU.mult, op1=ALU.subtract))
                            S.op('dve', [sk, 'rc'], [('cct', 0)], lambda v, p0=p0, p1=p1, src=src: v.tensor_tensor(
                                out=cct[0][p0:p1, 0:16], in0=src[p0:p1, 16 + 112:16 + 128],
                                in1=rc[p0:p1, j * 16:(j + 1) * 16], op=ALU.mult))
                            S.op('dve', [('cct', 0), 'cA', 'cdbf'], ['cdbf'], lambda v, p0=p0, p1=p1: v.tensor_tensor(
                                out=dbf[p0:p1, 112:128], in0=cct[0][p0:p1, 0:16],
                                in1=V[p0:p1, 16 + 112:16 + 128], op=ALU.subtract))
                        for wi, (t0, W) in enumerate(WINS):
                            pi = 2 + wi % 2
                            S.op('pe', ['cdbf', 'wpd'], [('ps', pi)], lambda t, pi=pi: t.matmul(
                                psb[pi][:, 0:W], lhsT=wpd[:, j, :], rhs=dbf[:, t0:t0 + W], start=True, stop=True))
                            mk = [('mixT', jj) for jj in range(t0 // 128, (t0 + W) // 128)]
                            S.op('act', [('ps', pi), 'pp'], mk, lambda a, pi=pi: a.activation(
                                out=mixT[:, 6 + j, t0:t0 + W], in_=psb[pi][:, 0:W], func=AF.Copy,
                                scale=pp[:, 6 + j:7 + j]))
                    S.barrier()

                wbF = [sb("hwbF%d" % i, [128, 8, 512], BF16, mph) for i in range(2)]
                for hh in range(2):
                    for (cbase, dcol, bi) in ((C_Q, 0, 0), (C_G, 384, 3), (C_F, 128, 1), (C_I, 256, 2)):
                        wblock_load(wbF[hh], ('hwb', hh, bi), w_in, cbase + hh * 128, 128, dcol)
                with contextlib.ExitStack() as ph:
                    s_in = sb("s_in", [128, 4, 128], F32, mph)
                    sj = sb("sj", [128, 4, 128], F32, ph)
                    tt = sb("sjt", [128, 128], F32, ph)
                    dgl = sb("dgl", [128, 16], F32, ph)
                    dge = sb("dge", [128, 16], F32, ph)
                    sel = sb("sel_sb", [128, 4], F32, ph)
                    for r in range(4):
                        S.dma('sp', ['xout'], ['dgl'], lambda q, r=r: q.dma_start(
                            out=dgl[:, r * 4:(r + 1) * 4], in_=xout[r * 128:(r + 1) * 128, 512:516]))
                    S.dma('sp', [], ['sel'], lambda q: q.dma_start(out=sel[:, :], in_=sel_d[:, :]))
                    S.op('act', ['dgl'], ['dge'], lambda a: a.activation(
                        out=dge[:, :], in_=dgl[:, :], func=AF.Exp))
                    for h in range(4):
                        S.op('dve', [], [('sin', h)], lambda v, h=h: v.memset(s_in[:, h, :], 0.0))
                    sj3 = [sj, sb("sj1", [128, 4, 128], F32, ph), sb("sj2", [128, 4, 128], F32, ph)]
                    for r in range(3):
                        S.dma('sp', ['xout'], [('sj', r)], lambda q, r=r: q.dma_start(
                            out=sj3[r][:, :, :],
                            in_=xout[r * 128:(r + 1) * 128, 0:512].rearrange("d (h e) -> d h e", h=4)))
                    for r in range(3):
                        sj = sj3[r]
                        for h in range(4):
                            S.op('dve', [('sin', h), 'dge', ('sj', r)], ['sjt'], lambda v, h=h, r=r, sj=sj: v.scalar_tensor_tensor(
                                out=tt[:, :], in0=s_in[:, h, :], scalar=dge[:, r * 4 + h:r * 4 + h + 1],
                                in1=sj[:, h, :], op0=ALU.mult, op1=ALU.add))
                            S.op('dve', ['sjt', ('sin', h)], ['sjt'], lambda v, h=h: v.tensor_tensor(
                                out=tt[:, :], in0=tt[:, :], in1=s_in[:, h, :], op=ALU.subtract))
                            S.op('dve', ['sjt', 'sel', ('sin', h)], [('sin', h)], lambda v, h=h, r=r: v.scalar_tensor_tensor(
                                out=s_in[:, h, :], in0=tt[:, :], scalar=sel[:, r:r + 1],
                                in1=s_in[:, h, :], op0=ALU.mult, op1=ALU.add))
                    S.barrier()

                if stop == 'combine':
                    return finish()
                if stop == 'convpool':
                    return finish()
                with contextlib.ExitStack() as ph:
                    hgrn2(ph, True, mixT, s_in, wb=wbF, preloaded=(0, 1))
                    S.barrier()

                if stop == 'hg':
                    return finish()
                def layernorm_tiles(ph, gi, post):
                    gbc = sb("gbc%d" % gi, [128, D], F32, ph)
                    bbc = sb("bbc%d" % gi, [128, D], F32, ph)
                    S.dma('sp', [], ['gbc'], lambda q: q.dma_start(out=gbc[:, :], in_=lnp[gi, :, :]))
                    S.dma('sp', [], ['bbc'], lambda q: q.dma_start(out=bbc[:, :], in_=lnp[gi + 1, :, :]))
                    st = sb("lnst%d" % gi, [128, NTILE, 12], F32, ph)
                    mv = sb("lnmv%d" % gi, [128, NTILE, 2], F32, ph)
                    rs = sb("lnrs%d" % gi, [128, NTILE], F32, ph)
                    xn = [sb("lnxn%d_%d" % (gi, i), [128, D], F32, ph) for i in range(2)]
                    xb = [sb("lnxb%d_%d" % (gi, i), [128, D], BF16, ph) for i in range(2)]

                    def ln_stats(j):
                        for c in range(2):
                            S.op('dve', [('xres', j)], [('lnst', j)], lambda v, c=c: v.bn_stats(
                                out=st[:, j, c * 6:(c + 1) * 6], in_=x_res[:, j, c * 512:(c + 1) * 512]))
                        S.op('dve', [('lnst', j)], ['lnmv'], lambda v: v.bn_aggr(
                            out=mv[:, j, :], in_=st[:, j, :]))

                    def ln_apply(j0=0):
                        S.op('act', ['lnmv', 'epsb'], ['lnrs'], lambda a: a.activation(
                            out=rs[:, :].rearrange("p (t o) -> p t o", o=1), in_=mv[:, :, 1:2],
                            func=AF.Ln, bias=epsb[:, 1:2]))
                        S.op('act', ['lnrs'], ['lnrs'], lambda a: a.activation(
                            out=rs[:, :], in_=rs[:, :], func=AF.Exp, scale=-0.5))
                        for j in range(j0, NTILE):
                            b = j % 2
                            S.op('dve', [('xres', j), 'lnmv', 'gbc'], [('lnxn', b)], lambda v: v.scalar_tensor_tensor(
                                out=xn[b][:, :], in0=x_res[:, j, :], scalar=mv[:, j, 0:1], in1=gbc[:, :],
                                op0=ALU.subtract, op1=ALU.mult))
                            S.op('dve', [('lnxn', b), 'lnrs', 'bbc'], [('xres', j)], lambda v: v.scalar_tensor_tensor(
                                out=x_res[:, j, :], in0=xn[b][:, :], scalar=rs[:, j:j + 1], in1=bbc[:, :],
                                op0=ALU.mult, op1=ALU.add))
                            if j == 0:
                                S.op('dve', [('xres', 0), 'mask0'], [('xres', 0)], lambda v: v.tensor_scalar(
                                    out=x_res[:, 0, :], in0=x_res[:, 0, :], scalar1=mask0[:, 0:1], scalar2=None,
                                    op0=ALU.mult))

                        def cast(j):
                            S.op('act', [('xres', j)], [('lnxb', j % 2)], lambda a: a.activation(
                                out=xb[j % 2][:, :], in_=x_res[:, j, :], func=AF.Copy))

                        cast(j0)
                        for j in range(j0, NTILE):
                            src = xb[j % 2]
                            for k in range(8):
                                S.op('pe', [('lnxb', j % 2), 'ident'], [('pst',)], lambda t, k=k: t.transpose(
                                    out=pst[:, k * 128:(k + 1) * 128], in_=src[:, k * 128:(k + 1) * 128],
                                    identity=ident[:, :]))
                            if j + 1 < NTILE:
                                cast(j + 1)
                            S.op('act', [('pst',)], [('xT', j)], lambda a: a.activation(
                                out=xT[:, :, j * 128:(j + 1) * 128],
                                in_=pst[:, :].rearrange("p (k t) -> p k t", t=128), func=AF.Copy))
                    return ln_stats, ln_apply

                with contextlib.ExitStack() as ph:
                    wo = sb("wo", [128, 8, D], BF16, ph)
                    for half in range(2):
                        wblock_load(wo, 'wo', w_o, half * 512, 512, half * 512)
                    ln_stats, ln_apply = layernorm_tiles(ph, 0, None)
                    for j in range(NTILE):
                        for fb in range(2):
                            pi = (2 * j + fb) % 4
                            for k in range(8):
                                S.op('pe', [('mixT', j), 'wo'], [('ps', pi)], lambda t, k=k, pi=pi: t.matmul(
                                    psb[pi][:, :], lhsT=mixT[:, k, j * 128:(j + 1) * 128],
                                    rhs=wo[:, k, fb * 512:(fb + 1) * 512], start=(k == 0), stop=(k == 7)))
                            S.op('dve', [('xres', j), ('ps', pi)], [('xres', j)], lambda v, pi=pi: v.scalar_tensor_tensor(
                                out=x_res[:, j, fb * 512:(fb + 1) * 512], in0=x_res[:, j, fb * 512:(fb + 1) * 512],
                                scalar=ALPHA, in1=psb[pi][:, :], op0=ALU.mult, op1=ALU.add))
                        ln_stats(j)
                    ln_apply()
                    S.barrier()
            S.barrier()

            if stop == 'wo':
                return finish()
            FW = [(0, 512), (510, 512), (1020, 512), (1530, 512), (2040, 136)]
            fj0 = 0
            if layer == 1:
                FW = [(126, 412), (536, 412), (946, 412), (1356, 412), (1766, 410)]
                fj0 = 1
            with contextlib.ExitStack() as ph:
                gbuf = sb("gbuf", [128, 6, NT], BF16, ph)
                wdn = sb("wdn", [128, 6, D], BF16, ph)
                wup = [sb("wup%d" % i, [128, 8, 256], BF16, ph) for i in range(3)]
                ug = [sb("ug%d" % i, [128, 512], F32, ph) for i in range(2)]
                uv = [sb("uv%d" % i, [128, 512], F32, ph) for i in range(2)]
                sgt = [sb("sgt%d" % i, [128, 512], F32, ph) for i in range(2)]
                ln_stats2, ln_apply2 = layernorm_tiles(ph, 2, None)
                it = 0

                def load_wup(pc):
                    wblock_load(wup[pc % 3], ('wup', pc % 3, 0), w_up, pc * 128, 128, 0)
                    wblock_load(wup[pc % 3], ('wup', pc % 3, 1), w_up, DFF + pc * 128, 128, 128)

                for gi, (pc0, pc1) in enumerate(FFN_GROUPS):
                    npc = pc1 - pc0
                    S.dma('pool', [], ['wdn'], lambda q, pc0=pc0, npc=npc: q.dma_start(
                        out=wdn[:, 0:npc, :],
                        in_=w_down[pc0 * 128:pc1 * 128, :].rearrange("(c p) n -> p c n", p=128)))
                    S.op('dve', [], ['gbuf'], lambda v: v.memset(gbuf[:, :, 0:2], 0.0))
                    if gi == 0:
                        load_wup(0)
                        load_wup(1)
                    for pc in range(pc0, pc1):
                        wu = wup[pc % 3]
                        if pc + 2 < NPC:
                            load_wup(pc + 2)
                        cg = 19 + pc * 3
                        cv_ = 19 + (NPC + pc) * 3
                        bg = 19 + 132 + pc
                        bv = 19 + 132 + NPC + pc
                        for (c0, W) in FW:
                            b = it % 2
                            it += 1
                            pg, pv = psb[2 * b], psb[2 * b + 1]
                            kg, kv = ('ps', 2 * b), ('ps', 2 * b + 1)
                            xkeys = [('xT', jj) for jj in range(c0 // 128, (c0 + W + 127) // 128)]
                            for (ps_, pk, wc0) in ((pg, kg, 0), (pv, kv, 128)):
                                wuk = ('wup', pc % 3, wc0 // 128)
                                for k in range(8):
                                    S.op('pe', [wuk] + xkeys, [pk], lambda t, k=k, ps_=ps_, wc0=wc0: t.matmul(
                                        ps_[:, 0:W], lhsT=wu[:, k, wc0:wc0 + 128], rhs=xT[:, k, c0:c0 + W],
                                        start=(k == 0), stop=(k == 7)))
                            n = W - 2
                            for (ps_, pk, u, uk, tp, bb) in ((pg, kg, ug[b], ('ug', b), cg, bg),
                                                             (pv, kv, uv[b], ('uv', b), cv_, bv)):
                                S.op('act', [pk, 'pp'], [uk], lambda a, ps_=ps_, u=u, tp=tp, bb=bb: a.activation(
                                    out=u[:, 0:n], in_=ps_[:, 2:W], func=AF.Identity,
                                    bias=pp[:, bb:bb + 1], scale=pp[:, tp + 2:tp + 3]))
                                S.op('dve', [pk, uk, 'pp'], [uk], lambda v, ps_=ps_, u=u, tp=tp: v.scalar_tensor_tensor(
                                    out=u[:, 0:n], in0=ps_[:, 1:W - 1], scalar=pp[:, tp + 1:tp + 2],
                                    in1=u[:, 0:n], op0=ALU.mult, op1=ALU.add))
                                S.op('dve', [pk, uk, 'pp'], [uk], lambda v, ps_=ps_, u=u, tp=tp: v.scalar_tensor_tensor(
                                    out=u[:, 0:n], in0=ps_[:, 0:W - 2], scalar=pp[:, tp:tp + 1],
                                    in1=u[:, 0:n], op0=ALU.mult, op1=ALU.add))
                            S.op('act', [('ug', b)], [('sgt', b)], lambda a, b=b: a.activation(
                                out=sgt[b][:, 0:n], in_=ug[b][:, 0:n], func=AF.Silu))
                            S.op('dve', [('sgt', b), ('uv', b)], ['gbuf'], lambda g, b=b, pc=pc: g.tensor_tensor(
                                out=gbuf[:, pc - pc0, c0 + 2:c0 + W], in0=sgt[b][:, 0:n], in1=uv[b][:, 0:n],
                                op=ALU.mult))
                    for j in range(fj0, NTILE):
                        for fb in range(2):
                            pi = 4 + (2 * j + fb) % 3
                            for c in range(npc):
                                S.op('pe', ['gbuf', 'wdn'], [('ps', pi)], lambda t, c=c, pi=pi: t.matmul(
                                    psb[pi][:, :], lhsT=gbuf[:, c, j * 128:(j + 1) * 128],
                                    rhs=wdn[:, c, fb * 512:(fb + 1) * 512], start=(c == 0), stop=(c == npc - 1)))
                            xs = x_res[:, j, fb * 512:(fb + 1) * 512]
                            if gi == 0:
                                S.op('dve', [('xres', j), ('ps', pi)], [('xres', j)], lambda v, pi=pi, xs=xs: v.scalar_tensor_tensor(
                                    out=xs, in0=xs, scalar=ALPHA, in1=psb[pi][:, :], op0=ALU.mult, op1=ALU.add))
                            else:
                                S.op('dve', [('xres', j), ('ps', pi)], [('xres', j)], lambda v, pi=pi, xs=xs: v.tensor_tensor(
                                    out=xs, in0=xs, in1=psb[pi][:, :], op=ALU.add))
                        if gi == len(FFN_GROUPS) - 1:
                            ln_stats2(j)
                ln_apply2(fj0)
                S.barrier()
            S.barrier()

        for layer in range(2):
            r = emit_layer(layer)
            if r is not None:
                return r
        for j in range(1, NTILE):
            S.dma('sp', [('xres', j)], [], lambda q, j=j: q.dma_start(
                out=yw[(j - 1) * 128:j * 128, :], in_=x_res[:, j, :]))
        S.barrier()
    return nc


_PROGS = {}


def _prog(mode):
    if mode not in _PROGS:
        _PROGS[mode] = build_program(mode)
    return _PROGS[mode]


def _consts():
    c = np.zeros((128, 640), np.float32)
    c[:, 0:128] = np.eye(128, dtype=np.float32)
    s = np.arange(128)[:, None]
    t = np.arange(128)[None, :]
    m = ((t >= s) & ((t // 64) == (s // 64))).astype(np.float32)
    for j in range(4):
        c[:, 128 + j * 128:256 + j * 128] = m
    return c


def _pp(l, w_conv, pool_scale, hg_lower_bounds, hg_norm_g, w_ffn_conv, b_ffn_conv):
    p = np.zeros((128, NPP), np.float32)
    p[:, 0:6] = w_conv[l].reshape(2, 128, 3).transpose(1, 0, 2).reshape(128, 6)
    p[:, 6:8] = pool_scale[l].reshape(2, 128).T
    rw = np.zeros((128, 2), np.float32)
    rw[0:64, 0], rw[64:128, 0], rw[0:64, 1], rw[64:128, 1] = 1 / 2, 1 / 4, 1 / 8, 1 / 16
    p[:, 8:10] = rw
    p[:, 10:14] = hg_lower_bounds[0].reshape(4, 128).T
    p[:, 14:18] = hg_lower_bounds[1].reshape(4, 128).T
    p[:, 18] = hg_norm_g[l]
    p[:, 19:19 + 132] = w_ffn_conv[l].reshape(44, 128, 3).transpose(1, 0, 2).reshape(128, 132)
    p[:, 19 + 132:] = b_ffn_conv[l].reshape(44, 128).T
    return p


def _rc(first):
    r = np.zeros((128, 32), np.float32)
    wins = {(0, 0): 2, (0, 1): 4, (1, 0): 8, (1, 1): 16}
    for (j, hf), win in wins.items():
        for i in range(16):
            cnt = min(i + 1, win) if first else win
            r[hf * 64:(hf + 1) * 64, j * 16 + i] = 1.0 / cnt
    return r


def kernel(x, meta_tokens, hg_lower_bounds, w_in, w_conv, w_pool, pool_scale, hg_norm_g,
           w_o, ln1_g, ln1_b, w_up, w_ffn_conv, b_ffn_conv, w_down, ln2_g, ln2_b):
    f = lambda a: np.ascontiguousarray(np.asarray(a, dtype=np.float32))
    x, meta_tokens, hg_lower_bounds, w_in, w_conv, w_pool, pool_scale, hg_norm_g, w_o, ln1_g, \
        ln1_b, w_up, w_ffn_conv, b_ffn_conv, w_down, ln2_g, ln2_b = map(f, (
            x, meta_tokens, hg_lower_bounds, w_in, w_conv, w_pool, pool_scale, hg_norm_g, w_o,
            ln1_g, ln1_b, w_up, w_ffn_conv, b_ffn_conv, w_down, ln2_g, ln2_b))
    ncores = 8
    cores = list(range(ncores))
    consts = _consts()
    shared = {"consts": consts}
    for l in range(2):
        shared["w_in%d" % l] = w_in[l]
        shared["pp%d" % l] = _pp(l, w_conv, pool_scale, hg_lower_bounds, hg_norm_g, w_ffn_conv, b_ffn_conv)
        shared["w_o%d" % l] = w_o[l]
        shared["w_up%d" % l] = w_up[l]
        shared["w_down%d" % l] = w_down[l]
        shared["w_pool%d" % l] = w_pool[l]
        shared["lnp%d" % l] = np.ascontiguousarray(np.stack([
            np.broadcast_to(v[l][None, :], (128, D)) for v in (ln1_g, ln1_b, ln2_g, ln2_b)]))
    maps = []
    for c in cores:
        b, r = divmod(c, 4)
        own = x[b, r * OWN:(r + 1) * OWN]
        if r == 0:
            halo = np.zeros((HALO, D), np.float32)
            halo[HALO - 16:] = meta_tokens
        else:
            halo = x[b, r * OWN - HALO:r * OWN]
        sel = np.zeros((128, 4), np.float32)
        sel[:, :r] = 1.0
        m0 = np.ones((128, 1), np.float32)
        if r == 0:
            m0[:HALO - 16] = 0.0
        m = dict(shared)
        m.update({"xw": np.ascontiguousarray(np.concatenate([halo, own], axis=0)),
                  "rc": _rc(r == 0), "mask0": m0, "sel": sel})
        maps.append(m)
    res = run_bass_kernel_spmd(_prog('fused'), maps, core_ids=cores)
    out = np.zeros((2, 4 * OWN, D), np.float32)
    for c in cores:
        b, r = divmod(c, 4)
        out[b, r * OWN:(r + 1) * OWN] = res.results[c]["yw"]
    return out
```
